# Optimizing a Trainium2 kernel written in Bass

```python
import jax, jax.numpy as jnp
from jax import lax
import numpy as np

D_MODEL = 2048
BATCH = 1
SEQ = 8192
DEPTH = 4

N_MIXERS = 3
N_HGRN_LAYERS = (DEPTH + N_MIXERS - 1) // N_MIXERS
DEEPNORM_ALPHA = (2 * DEPTH) ** 0.25
DEEPNORM_BETA = (8 * DEPTH) ** -0.25
LN_EPS = 1e-5
RMS_EPS = 1e-6
ROPE_THETA = 500000.0
ROPE_FRACTION = 4

HG_HEAD_DIM = 128
HG_HEADS = D_MODEL // HG_HEAD_DIM
HG_CHUNK = 64

NSA_HEAD_DIM = 128
NSA_HEADS = D_MODEL // NSA_HEAD_DIM
NSA_KV_GROUPS = 4
CMP_BLOCK = 32
CMP_STRIDE = 16
CMP_HIDDEN = 256
SLC_BLOCK = 64
SLC_TOPK = 16
NSA_WINDOW = 512
NSA_Q_BLOCK = 64
FORCE_BONUS = 1.0e4

SWA_HEAD_DIM = 64
SWA_HEADS = D_MODEL // SWA_HEAD_DIM
SWA_KV_HEADS = 4
SWA_WINDOW = 128

HG_IN = 4 * D_MODEL
NSA_IN = NSA_HEADS * NSA_HEAD_DIM + 6 * NSA_KV_GROUPS * NSA_HEAD_DIM + 3 * NSA_HEADS + D_MODEL
SWA_IN = SWA_HEADS * SWA_HEAD_DIM + 2 * SWA_KV_HEADS * SWA_HEAD_DIM + D_MODEL

kernel_name = "hybrid_hgrn2_nsa_swa_sink_deepnorm"

F32 = jnp.float32


def _layer_norm(x, g, b):
    xf = x.astype(F32)
    mu = jnp.mean(xf, -1, keepdims=True)
    xc = xf - mu
    var = jnp.mean(xc * xc, -1, keepdims=True)
    return (xc * lax.rsqrt(var + LN_EPS) * g.astype(F32) + b.astype(F32)).astype(x.dtype)


def _rotary_tables(positions, head_dim):
    rot = head_dim // ROPE_FRACTION
    inv_freq = ROPE_THETA ** (-jnp.arange(0, rot, 2, dtype=F32) / rot)
    ang = positions.astype(F32)[..., None] * inv_freq
    return jnp.cos(ang), jnp.sin(ang)


def _partial_rotary(t, cos, sin):
    half = cos.shape[-1]
    tf = t.astype(F32)
    c, s = cos[:, :, None, :], sin[:, :, None, :]
    x1, x2 = tf[..., :half], tf[..., half:2 * half]
    return jnp.concatenate([x1 * c - x2 * s, x2 * c + x1 * s, tf[..., 2 * half:]], -1).astype(t.dtype)


def _masked_softmax(scores, mask):
    s = jnp.where(mask, scores.astype(F32), -jnp.inf)
    m = jnp.max(s, -1, keepdims=True)
    m = jnp.where(jnp.isfinite(m), m, 0.0)
    e = jnp.exp(s - m)
    return e / jnp.maximum(jnp.sum(e, -1, keepdims=True), 1e-30)


def _hgrn_lower_bounds(lb_logits):
    lb = jnp.cumsum(jax.nn.softmax(lb_logits.astype(F32), axis=0), axis=0)
    return lb - lb[0:1]


def _hgrn2_mixer(x, w_in, g_norm, w_out, lower_bound):
    B, S, D = x.shape
    H, Dk, C = HG_HEADS, HG_HEAD_DIM, HG_CHUNK
    n_chunks = S // C
    q, f_logit, v, z = jnp.split(x @ w_in, 4, axis=-1)
    log_f = jnp.logaddexp(jnp.log(lower_bound),
                          jnp.log1p(-lower_bound) + jax.nn.log_sigmoid(f_logit.astype(F32)))
    k = -jnp.expm1(log_f)

    def chunks(t):
        return t.astype(F32).reshape(B, n_chunks, C, H, Dk).transpose(1, 0, 3, 2, 4)

    causal = jnp.tril(jnp.ones((C, C), dtype=bool))[:, :, None]

    def step(state, inp):
        q_c, k_c, v_c, g_c = inp
        b = jnp.cumsum(g_c, axis=2)
        o_inter = jnp.einsum('bhtk,bhkv->bhtv', q_c * jnp.exp(b), state)
        rel = jnp.where(causal, b[:, :, :, None, :] - b[:, :, None, :, :], -jnp.inf)
        a = jnp.einsum('bhtk,bhtsk->bhts', q_c, jnp.exp(rel) * k_c[:, :, None, :, :])
        o = o_inter + jnp.einsum('bhts,bhsv->bhtv', a, v_c)
        b_end = b[:, :, -1:, :]
        state = state * jnp.exp(b_end[:, :, 0, :, None]) + jnp.einsum(
            'bhsk,bhsv->bhkv', k_c * jnp.exp(b_end - b), v_c)
        return state, o

    state0 = jnp.zeros((B, H, Dk, Dk), F32)
    _, o = lax.scan(step, state0, (chunks(q), chunks(k), chunks(v), chunks(log_f)))
    o = o.transpose(1, 0, 3, 2, 4).reshape(B, S, H, Dk)
    o = o * lax.rsqrt(jnp.mean(o * o, -1, keepdims=True) + RMS_EPS)
    o = o.reshape(B, S, D) * g_norm.astype(F32) * jax.nn.silu(z.astype(F32))
    return o.astype(x.dtype) @ w_out


def _nsa_mixer(x, cos, sin, w_in, cmp_pos_k, cmp_w1_k, cmp_w2_k,
               cmp_pos_v, cmp_w1_v, cmp_w2_v, w_out):
    B, S, D = x.shape
    H, G, Dh = NSA_HEADS, NSA_KV_GROUPS, NSA_HEAD_DIM
    R = H // G
    QB = NSA_Q_BLOCK
    widths = [H * Dh] + [G * Dh] * 6 + [3 * H]
    q, k_c, v_c, k_s, v_s, k_w, v_w, gate_logit, z = jnp.split(
        x @ w_in, [int(c) for c in np.cumsum(widths)], axis=-1)
    q = _partial_rotary(q.reshape(B, S, H, Dh), cos, sin)
    k_c, k_s, k_w = (_partial_rotary(t.reshape(B, S, G, Dh), cos, sin) for t in (k_c, k_s, k_w))
    v_c, v_s, v_w = (t.reshape(B, S, G, Dh) for t in (v_c, v_s, v_w))

    n_cmp = S // CMP_STRIDE - 1

    def compress(t, pos, w1, w2):
        c = t.reshape(B, S // CMP_STRIDE, CMP_STRIDE, G, Dh)
        blk = jnp.concatenate([c[:, :-1], c[:, 1:]], axis=2) + pos[:, None, :]
        flat = blk.transpose(0, 1, 3, 2, 4).reshape(B, n_cmp, G, CMP_BLOCK * Dh)
        return jax.nn.silu(flat @ w1) @ w2

    kc_blk = compress(k_c, cmp_pos_k, cmp_w1_k, cmp_w2_k)
    vc_blk = compress(v_c, cmp_pos_v, cmp_w1_v, cmp_w2_v)
    cmp_end = jnp.arange(n_cmp) * CMP_STRIDE + CMP_BLOCK - 1

    n_slc = S // SLC_BLOCK
    top_k = min(SLC_TOPK, n_slc)
    ratio = SLC_BLOCK // CMP_STRIDE
    i_idx = np.arange(n_cmp)[:, None]
    j_idx = np.arange(n_slc)[None, :]
    agg = jnp.asarray((i_idx >= ratio * j_idx - CMP_BLOCK // CMP_STRIDE + 1)
                      & (i_idx <= ratio * j_idx + ratio - 1), F32)
    ks_blk = k_s.reshape(B, n_slc, SLC_BLOCK, G, Dh).transpose(0, 3, 1, 2, 4)
    vs_blk = v_s.reshape(B, n_slc, SLC_BLOCK, G, Dh).transpose(0, 3, 1, 2, 4)
    gather = jax.vmap(jax.vmap(lambda t, ix: t[ix]))

    kw_pad = jnp.pad(k_w, ((0, 0), (NSA_WINDOW, 0), (0, 0), (0, 0)))
    vw_pad = jnp.pad(v_w, ((0, 0), (NSA_WINDOW, 0), (0, 0), (0, 0)))
    scale = Dh ** -0.5
    n_qb = S // QB
    q_blocks = q.reshape(B, n_qb, QB, G, R, Dh).transpose(1, 0, 2, 3, 4, 5)
    g_blocks = jax.nn.sigmoid(gate_logit.astype(F32)).reshape(B, n_qb, QB, 3, H).transpose(1, 0, 2, 3, 4)
    j = jnp.arange(n_slc)
    w_off = jnp.arange(QB + NSA_WINDOW)

    def attend_block(args):
        blk, qb, gb = args
        start = blk * QB
        t_pos = start + jnp.arange(QB)
        s_c = jnp.einsum('bqgrd,bngd->bgrqn', qb, kc_blk) * scale
        p_c = _masked_softmax(s_c, cmp_end[None, :] <= t_pos[:, None])
        o_cmp = jnp.einsum('bgrqn,bngd->bqgrd', p_c, vc_blk)
        imp = jnp.einsum('bgrqn,nj->bgqj', p_c, agg)
        cur = t_pos[:, None] // SLC_BLOCK
        allowed = j[None, :] * SLC_BLOCK <= t_pos[:, None]
        forced = (j[None, :] == 0) | (j[None, :] == cur) | (j[None, :] == cur - 1)
        imp = jnp.where(allowed, imp + jnp.where(forced, FORCE_BONUS, 0.0), -jnp.inf)
        _, sel = lax.top_k(imp, top_k)
        k_sel = gather(ks_blk, sel)
        v_sel = gather(vs_blk, sel)
        kpos = sel[..., None] * SLC_BLOCK + jnp.arange(SLC_BLOCK)
        m_s = (kpos <= t_pos[:, None, None])[:, :, None]
        s_s = jnp.einsum('bqgrd,bgqkld->bgrqkl', qb, k_sel) * scale
        p_s = _masked_softmax(s_s.reshape(B, G, R, QB, -1), m_s.reshape(B, G, 1, QB, -1))
        o_slc = jnp.einsum('bgrqkl,bgqkld->bqgrd', p_s.reshape(s_s.shape), v_sel)
        kw = lax.dynamic_slice_in_dim(kw_pad, start, QB + NSA_WINDOW, axis=1)
        vw = lax.dynamic_slice_in_dim(vw_pad, start, QB + NSA_WINDOW, axis=1)
        wpos = start - NSA_WINDOW + w_off
        diff = t_pos[:, None] - wpos[None, :]
        m_w = (diff >= 0) & (diff < NSA_WINDOW) & (wpos[None, :] >= 0)
        s_w = jnp.einsum('bqgrd,bkgd->bgrqk', qb, kw) * scale
        o_win = jnp.einsum('bgrqk,bkgd->bqgrd', _masked_softmax(s_w, m_w), vw)
        gate = lambda i: gb[:, :, i].reshape(B, QB, G, R, 1)
        o = gate(0) * o_cmp + gate(1) * o_slc + gate(2) * o_win
        return o.reshape(B, QB, H * Dh)

    o = lax.map(attend_block, (jnp.arange(n_qb), q_blocks, g_blocks))
    o = o.transpose(1, 0, 2, 3).reshape(B, S, H * Dh)
    return (o * jax.nn.silu(z.astype(F32))).astype(x.dtype) @ w_out


def _swa_sink_mixer(x, cos, sin, w_in, sinks, w_out):
    B, S, D = x.shape
    H, KV, Dh, W = SWA_HEADS, SWA_KV_HEADS, SWA_HEAD_DIM, SWA_WINDOW
    R = H // KV
    n_blk = S // W
    q, k, v, z = jnp.split(x @ w_in, [H * Dh, H * Dh + KV * Dh, H * Dh + 2 * KV * Dh], axis=-1)
    q = _partial_rotary(q.reshape(B, S, H, Dh), cos, sin).reshape(B, n_blk, W, KV, R, Dh)
    k = _partial_rotary(k.reshape(B, S, KV, Dh), cos, sin).reshape(B, n_blk, W, KV, Dh)
    v = v.reshape(B, n_blk, W, KV, Dh)

    def with_prev(t):
        prev = jnp.concatenate([jnp.zeros_like(t[:, :1]), t[:, :-1]], axis=1)
        return jnp.concatenate([prev, t], axis=2)

    kk, vv = with_prev(k), with_prev(v)
    q_off = jnp.arange(W)[:, None]
    k_off = jnp.arange(2 * W)[None, :] - W
    diff = q_off - k_off
    band = (diff >= 0) & (diff < W)
    real = (jnp.arange(n_blk) > 0)[:, None, None] | (k_off >= 0)[None]
    mask = (band[None] & real)[None, :, None, None]
    s = jnp.einsum('bnqgrd,bnkgd->bngrqk', q, kk).astype(F32) * (Dh ** -0.5)
    s = jnp.where(mask, s, -jnp.inf)
    sink = jnp.broadcast_to(sinks.astype(F32).reshape(1, 1, KV, R, 1, 1), s.shape[:-1] + (1,))
    p = jax.nn.softmax(jnp.concatenate([s, sink], axis=-1), axis=-1)[..., :-1]
    o = jnp.einsum('bngrqk,bnkgd->bnqgrd', p, vv).reshape(B, S, H * Dh)
    return (o * jax.nn.silu(z.astype(F32))).astype(x.dtype) @ w_out


def setup_inputs(seed: int = 0) -> dict:
    key = jax.random.key(seed)
    ks = iter(jax.random.split(key, 48))
    D = D_MODEL

    def dense(fan_in, fan_out, scale=1.0):
        return jax.random.normal(next(ks), (fan_in, fan_out), F32) * (fan_in ** -0.5) * scale

    def gain(n):
        return 1.0 + 0.02 * jax.random.normal(next(ks), (n,), F32)

    def small(shape):
        return 0.02 * jax.random.normal(next(ks), shape, F32)

    x = jax.random.normal(next(ks), (BATCH, SEQ, D), F32)
    positions = (jax.random.randint(next(ks), (BATCH, 1), 0, 4096, dtype=jnp.int32)
                 + jnp.arange(SEQ, dtype=jnp.int32)[None, :])
    hgrn_lb_logits = 0.5 * jax.random.normal(next(ks), (N_HGRN_LAYERS, D), F32)
    cmp_in = CMP_BLOCK * NSA_HEAD_DIM
    return {
        "x": x,
        "positions": positions,
        "hgrn_lb_logits": hgrn_lb_logits,
        "l0_w_in": dense(D, HG_IN),
        "l0_g_norm": gain(D),
        "l0_w_out": dense(D, D, DEEPNORM_BETA),
        "l0_ln_g": gain(D),
        "l0_ln_b": small((D,)),
        "l1_w_in": dense(D, NSA_IN),
        "l1_cmp_pos_k": small((CMP_BLOCK, NSA_HEAD_DIM)),
        "l1_cmp_w1_k": dense(cmp_in, CMP_HIDDEN),
        "l1_cmp_w2_k": dense(CMP_HIDDEN, NSA_HEAD_DIM),
        "l1_cmp_pos_v": small((CMP_BLOCK, NSA_HEAD_DIM)),
        "l1_cmp_w1_v": dense(cmp_in, CMP_HIDDEN),
        "l1_cmp_w2_v": dense(CMP_HIDDEN, NSA_HEAD_DIM),
        "l1_w_out": dense(D, D, DEEPNORM_BETA),
        "l1_ln_g": gain(D),
        "l1_ln_b": small((D,)),
        "l2_w_in": dense(D, SWA_IN),
        "l2_sinks": 0.5 * jax.random.normal(next(ks), (SWA_HEADS,), F32),
        "l2_w_out": dense(D, D, DEEPNORM_BETA),
        "l2_ln_g": gain(D),
        "l2_ln_b": small((D,)),
        "l3_w_in": dense(D, HG_IN),
        "l3_g_norm": gain(D),
        "l3_w_out": dense(D, D, DEEPNORM_BETA),
        "l3_ln_g": gain(D),
        "l3_ln_b": small((D,)),
    }


def reference(x, positions, hgrn_lb_logits,
              l0_w_in, l0_g_norm, l0_w_out, l0_ln_g, l0_ln_b,
              l1_w_in, l1_cmp_pos_k, l1_cmp_w1_k, l1_cmp_w2_k,
              l1_cmp_pos_v, l1_cmp_w1_v, l1_cmp_w2_v, l1_w_out, l1_ln_g, l1_ln_b,
              l2_w_in, l2_sinks, l2_w_out, l2_ln_g, l2_ln_b,
              l3_w_in, l3_g_norm, l3_w_out, l3_ln_g, l3_ln_b):
    layer_params = (
        (l0_w_in, l0_g_norm, l0_w_out, l0_ln_g, l0_ln_b),
        (l1_w_in, l1_cmp_pos_k, l1_cmp_w1_k, l1_cmp_w2_k,
         l1_cmp_pos_v, l1_cmp_w1_v, l1_cmp_w2_v, l1_w_out, l1_ln_g, l1_ln_b),
        (l2_w_in, l2_sinks, l2_w_out, l2_ln_g, l2_ln_b),
        (l3_w_in, l3_g_norm, l3_w_out, l3_ln_g, l3_ln_b),
    )
    lower_bounds = _hgrn_lower_bounds(hgrn_lb_logits)
    cos_nsa, sin_nsa = _rotary_tables(positions, NSA_HEAD_DIM)
    cos_swa, sin_swa = _rotary_tables(positions, SWA_HEAD_DIM)
    h = x
    for i in range(DEPTH):
        p = layer_params[i]
        kind = i % N_MIXERS
        if kind == 0:
            y = _hgrn2_mixer(h, p[0], p[1], p[2], lower_bounds[i // N_MIXERS])
        elif kind == 1:
            y = _nsa_mixer(h, cos_nsa, sin_nsa, *p[:-2])
        else:
            y = _swa_sink_mixer(h, cos_swa, sin_swa, *p[:-2])
        h = _layer_norm(DEEPNORM_ALPHA * h + y, p[-2], p[-1])
    return h
```

```python
import numpy as np
import ml_dtypes
import concourse.bass as bass
import concourse.mybir as mybir
from concourse.bass_utils import run_bass_kernel_spmd

F32 = mybir.dt.float32
BF16 = mybir.dt.bfloat16
I32 = mybir.dt.int32
AF = mybir.ActivationFunctionType
ALU = mybir.AluOpType
AX = mybir.AxisListType

NCORES = 8
D = 2048
S = 8192
DEPTH = 4
ALPHA = (2 * DEPTH) ** 0.25
LN_EPS = 1e-5
RMS_EPS = 1e-6
BIG = 32768.0


class Prog:
    CE = ("pe", "act", "dve", "pool")

    def __init__(self, nc, n_dma_sems=24, same_eng_sync=True):
        self.nc = nc
        self.same = same_eng_sync
        self.sem = {e: nc.alloc_semaphore(name=f"s_{e}") for e in self.CE}
        self.cnt = {e: 0 for e in self.CE}
        self.dsem = [nc.alloc_semaphore(name=f"d{i}") for i in range(n_dma_sems)]
        self.dcnt = [0] * n_dma_sems
        self.rr = 0
        self.seen = {e: {} for e in self.CE + ("sp",)}
        self.streams = {e: [] for e in self.CE + ("sp",)}
        self.buf = {}
        self.ctx = []

    def sbuf(self, name, shape, dt):
        g = self.nc.sbuf_tensor(name, list(shape), dt)
        t = g.__enter__()
        self.ctx.append(g)
        return t

    def psum(self, name, shape, dt=F32):
        g = self.nc.psum_tensor(name, list(shape), dt)
        t = g.__enter__()
        self.ctx.append(g)
        return t

    def _deps(self, eng, reads, writes):
        deps = {}
        def add(ev):
            if ev is None:
                return
            sk, v = ev
            if sk == eng and (eng == "pe" or not self.same):
                return
            if deps.get(sk, 0) < v:
                deps[sk] = v
        for k in reads:
            b = self.buf.get(k)
            if b:
                add(b[0])
        for k in writes:
            b = self.buf.get(k)
            if b:
                add(b[0])
                for r in b[1]:
                    add(r)
        waits = []
        for sk, v in deps.items():
            if self.seen[eng].get(sk, 0) < v:
                self.seen[eng][sk] = v
                waits.append((sk, v))
        return waits

    def _record(self, ev, reads, writes):
        for k in reads:
            b = self.buf.setdefault(k, [None, []])
            b[1].append(ev)
        for k in writes:
            self.buf[k] = [ev, []]

    def op(self, eng, fn, reads=(), writes=()):
        waits = self._deps(eng, reads, writes)
        self.cnt[eng] += 1
        ev = (eng, self.cnt[eng])
        self.streams[eng].append((waits, fn, eng))
        self._record(ev, reads, writes)
        return ev

    def dma(self, out, in_, reads=(), writes=(), q="sp", **kw):
        i = self.rr
        self.rr = (self.rr + 1) % len(self.dsem)
        waits = self._deps(q, reads, writes)
        if self.dcnt[i] > 0:
            sk, v = ("dma", i), 16 * self.dcnt[i]
            if self.seen[q].get(sk, 0) < v:
                self.seen[q][sk] = v
                waits.append((sk, v))
        self.dcnt[i] += 1
        ev = (("dma", i), 16 * self.dcnt[i])
        self.streams[q].append((waits, lambda e: e.dma_start(out=out, in_=in_, **kw), ("dma", i)))
        self._record(ev, reads, writes)
        return ev

    def _semh(self, sk):
        return self.dsem[sk[1]] if isinstance(sk, tuple) else self.sem[sk]

    def finish(self):
        nc = self.nc
        final_waits = [(("dma", i), 16 * c) for i, c in enumerate(self.dcnt) if c > 0]
        streams = self.streams
        semh = self._semh
        sems = self.sem
        dsem = self.dsem

        def replay(e, name):
            for waits, fn, inc in streams[name]:
                for sk, v in waits:
                    e.wait_ge(semh(sk), v)
                ins = fn(e)
                if isinstance(inc, tuple):
                    ins.then_inc(dsem[inc[1]], 16)
                else:
                    ins.then_inc(sems[inc], 1)

        with nc.Block() as block:
            @block.sync
            def _(e):
                replay(e, "sp")
                for sk, v in final_waits:
                    e.wait_ge(semh(sk), v)

            @block.tensor
            def _(e):
                replay(e, "pe")

            @block.scalar
            def _(e):
                replay(e, "act")

            @block.vector
            def _(e):
                replay(e, "dve")

            @block.gpsimd
            def _(e):
                replay(e, "pool")
        for g in reversed(self.ctx):
            g.__exit__(None, None, None)


def _bf16(a):
    return np.ascontiguousarray(a).astype(ml_dtypes.bfloat16)


def build_out_prog(ntok=1024):
    nc = bass.Bass("TRN2", target_bir_lowering=False)
    ogT = nc.dram_tensor("ogT", [D, ntok], BF16, kind="ExternalInput").ap()
    h = nc.dram_tensor("h", [ntok, D], F32, kind="ExternalInput").ap()
    w = nc.dram_tensor("w", [D, D], F32, kind="ExternalInput").ap()
    g = nc.dram_tensor("g", [1, D], F32, kind="ExternalInput").ap()
    b = nc.dram_tensor("b", [1, D], F32, kind="ExternalInput").ap()
    out = nc.dram_tensor("out", [ntok, D], F32, kind="ExternalOutput").ap()
    P = Prog(nc)
    KT = D // 128
    NT = ntok // 128
    og_sb = P.sbuf("og_sb", [128, KT, ntok], BF16)
    w_bf = P.sbuf("w_bf", [128, KT, D], BF16)
    wst = [P.sbuf(f"wst{i}", [128, 2, D], F32) for i in range(2)]
    g_sb = P.sbuf("g_sb", [128, D], F32)
    b_sb = P.sbuf("b_sb", [128, D], F32)
    hb = [P.sbuf(f"hb{i}", [128, D], F32) for i in range(2)]
    rb = [P.sbuf(f"rb{i}", [128, D], F32) for i in range(2)]
    st = [P.sbuf(f"st{i}", [128, 4, 6], F32) for i in range(2)]
    mv = [P.sbuf(f"mv{i}", [128, 2], F32) for i in range(2)]
    rs = [P.sbuf(f"rs{i}", [128, 1], F32) for i in range(2)]
    ps = [P.psum(f"ps{i}", [128, 512]) for i in range(4)]
    eps_sb = P.sbuf("eps_sb", [128, 1], F32)
    P.op("dve", lambda e: e.memset(eps_sb[:], LN_EPS), writes=["eps"])

    P.dma(og_sb[:], ogT.rearrange("(k p) t -> p k t", p=128), writes=["og"])
    P.dma(g_sb[:], g.partition_broadcast(128), writes=["g"])
    P.dma(b_sb[:], b.partition_broadcast(128), writes=["b"])
    wv = w.rearrange("(k p) n -> p k n", p=128)
    for k2 in range(KT // 2):
        s = k2 % 2
        P.dma(wst[s][:], wv[:, 2 * k2:2 * k2 + 2, :], writes=[f"wst{s}"])
        eng = "pool" if k2 % 2 else "act"
        if eng == "act":
            P.op("act", lambda e, s=s, k2=k2: e.copy(out=w_bf[:, 2 * k2:2 * k2 + 2, :], in_=wst[s][:]),
                 reads=[f"wst{s}"], writes=[f"wbf{k2}"])
        else:
            P.op("pool", lambda e, s=s, k2=k2: e.tensor_copy(out=w_bf[:, 2 * k2:2 * k2 + 2, :], in_=wst[s][:]),
                 reads=[f"wst{s}"], writes=[f"wbf{k2}"])
    wkeys = [f"wbf{k2}" for k2 in range(KT // 2)]
    for t in range(NT):
        s = t % 2
        P.dma(hb[s][:], h[t * 128:(t + 1) * 128, :], writes=[f"hb{s}"])
        for n in range(4):
            for k in range(KT):
                P.op("pe", lambda e, n=n, k=k, t=t: e.matmul(
                    ps[n][:], lhsT=og_sb[:, k, t * 128:(t + 1) * 128], rhs=w_bf[:, k, n * 512:(n + 1) * 512],
                    start=(k == 0), stop=(k == KT - 1)),
                    reads=["og"] + wkeys, writes=[f"ps{n}"])
            P.op("dve", lambda e, n=n, s=s: e.scalar_tensor_tensor(
                out=rb[s][:, n * 512:(n + 1) * 512], in0=hb[s][:, n * 512:(n + 1) * 512], scalar=ALPHA,
                in1=ps[n][:], op0=ALU.mult, op1=ALU.add),
                reads=[f"hb{s}", f"ps{n}"], writes=[f"rb{s}_{n}"])
            P.op("dve", lambda e, n=n, s=s: e.bn_stats(out=st[s][:, n, :], in_=rb[s][:, n * 512:(n + 1) * 512]),
                 reads=[f"rb{s}_{n}"], writes=[f"st{s}_{n}"])
        rkeys = [f"rb{s}_{n}" for n in range(4)]
        P.op("dve", lambda e, s=s: e.bn_aggr(out=mv[s][:], in_=st[s][:]),
             reads=[f"st{s}_{n}" for n in range(4)], writes=[f"mv{s}"])
        P.op("act", lambda e, s=s: e.activation(out=rs[s][:], in_=mv[s][:, 1:2], func=AF.Sqrt, bias=eps_sb[:, 0:1], scale=1.0),
             reads=[f"mv{s}", "eps"], writes=[f"rs{s}"])
        P.op("dve", lambda e, s=s: e.reciprocal(out=rs[s][:], in_=rs[s][:]),
             reads=[f"rs{s}"], writes=[f"rs{s}"])
        P.op("dve", lambda e, s=s: e.tensor_scalar(out=rb[s][:], in0=rb[s][:], scalar1=mv[s][:, 0:1],
                                                  scalar2=rs[s][:, 0:1], op0=ALU.subtract, op1=ALU.mult),
             reads=rkeys + [f"mv{s}", f"rs{s}"], writes=rkeys)
        P.op("pool", lambda e, s=s: e.tensor_tensor(out=rb[s][:], in0=rb[s][:], in1=g_sb[:], op=ALU.mult),
             reads=rkeys + ["g"], writes=rkeys)
        P.op("pool", lambda e, s=s: e.tensor_tensor(out=rb[s][:], in0=rb[s][:], in1=b_sb[:], op=ALU.add),
             reads=rkeys + ["b"], writes=rkeys)
        P.dma(out[t * 128:(t + 1) * 128, :], rb[s][:], reads=rkeys)
    P.finish()
    return nc


_PROGS = {}


def _get(name, builder):
    if name not in _PROGS:
        _PROGS[name] = builder()
    return _PROGS[name]


def run_out(ogT_full, h_full, w, g, b):
    nc = _get("out", build_out_prog)
    maps = []
    for c in range(NCORES):
        maps.append({
            "ogT": np.ascontiguousarray(ogT_full[:, c * 1024:(c + 1) * 1024]),
            "h": np.ascontiguousarray(h_full[c * 1024:(c + 1) * 1024]),
            "w": w, "g": g.reshape(1, D), "b": b.reshape(1, D),
        })
    res = run_bass_kernel_spmd(nc, maps, core_ids=list(range(NCORES)))
    return np.concatenate([r["out"] for r in res.results], axis=0)


TWO_PI = 2.0 * np.pi
CW1 = 6.28125
CW2 = float(np.float32(TWO_PI - 6.28125))


def emit_rotary_tables(P, pos_dram, half, rot, cos_sb, sin_sb, tag):
    NT = cos_sb.shape[1]
    posi = P.sbuf(f"posi{tag}", [128, NT], I32)
    posf = P.sbuf(f"posf{tag}", [128, NT], F32)
    ang = P.sbuf(f"ang{tag}", [128, NT, half], F32)
    yy = P.sbuf(f"yy{tag}", [128, NT, half], F32)
    ni = P.sbuf(f"ni{tag}", [128, NT, half], I32)
    nf = P.sbuf(f"nf{tag}", [128, NT, half], F32)
    rr = P.sbuf(f"rr{tag}", [128, NT, half], F32)
    mk = P.sbuf(f"mk{tag}", [128, NT, half], F32)
    k = f"rot{tag}"
    P.dma(posi[:], pos_dram, writes=[k + "posi"])
    P.op("dve", lambda e: e.tensor_copy(out=posf[:], in_=posi[:]), reads=[k + "posi"], writes=[k + "posf"])
    inv = [float(np.float32(500000.0) ** np.float32(-(2.0 * i) / rot)) for i in range(half)]
    for i in range(half):
        P.op("dve", lambda e, i=i: e.tensor_scalar(out=ang[:, :, i], in0=posf[:], scalar1=inv[i], scalar2=None,
                                                   op0=ALU.mult), reads=[k + "posf"], writes=[k + "ang"])

    def reduce_to(dst, src_key, shift):
        P.op("dve", lambda e: e.tensor_scalar(out=yy[:], in0=ang[:], scalar1=shift, scalar2=1.0 / TWO_PI,
                                              op0=ALU.add, op1=ALU.mult), reads=[k + "ang"], writes=[k + "yy"])
        P.op("dve", lambda e: e.tensor_copy(out=ni[:], in_=yy[:]), reads=[k + "yy"], writes=[k + "ni"])
        P.op("dve", lambda e: e.tensor_copy(out=nf[:], in_=ni[:]), reads=[k + "ni"], writes=[k + "nf"])
        P.op("dve", lambda e: e.scalar_tensor_tensor(out=rr[:], in0=nf[:], scalar=-CW1, in1=ang[:],
                                                     op0=ALU.mult, op1=ALU.add),
             reads=[k + "nf", k + "ang"], writes=[k + "rr"])
        P.op("dve", lambda e: e.scalar_tensor_tensor(out=rr[:], in0=nf[:], scalar=-CW2, in1=rr[:],
                                                     op0=ALU.mult, op1=ALU.add),
             reads=[k + "nf", k + "rr"], writes=[k + "rr"])
        if shift != 0.0:
            P.op("dve", lambda e: e.tensor_scalar(out=rr[:], in0=rr[:], scalar1=shift, scalar2=None, op0=ALU.add),
                 reads=[k + "rr"], writes=[k + "rr"])
        P.op("dve", lambda e: e.tensor_scalar(out=mk[:], in0=rr[:], scalar1=float(np.pi), scalar2=-TWO_PI,
                                              op0=ALU.is_gt, op1=ALU.mult), reads=[k + "rr"], writes=[k + "mk"])
        P.op("dve", lambda e: e.tensor_tensor(out=rr[:], in0=rr[:], in1=mk[:], op=ALU.add),
             reads=[k + "rr", k + "mk"], writes=[k + "rr"])
        P.op("dve", lambda e: e.tensor_scalar(out=mk[:], in0=rr[:], scalar1=-float(np.pi), scalar2=TWO_PI,
                                              op0=ALU.is_lt, op1=ALU.mult), reads=[k + "rr"], writes=[k + "mk"])
        P.op("dve", lambda e: e.tensor_tensor(out=rr[:], in0=rr[:], in1=mk[:], op=ALU.add),
             reads=[k + "rr", k + "mk"], writes=[k + "rr"])
        P.op("dve", lambda e: e.tensor_scalar(out=rr[:], in0=rr[:], scalar1=3.1415925, scalar2=-3.1415925,
                                              op0=ALU.min, op1=ALU.max), reads=[k + "rr"], writes=[k + "rr"])
        P.op("act", lambda e: e.activation(out=dst[:], in_=rr[:], func=AF.Sin), reads=[k + "rr"], writes=[src_key])

    reduce_to(sin_sb, k + "sin", 0.0)
    reduce_to(cos_sb, k + "cos", float(np.pi / 2))
    return k + "cos", k + "sin"


def emit_rotary(P, eng, src, dst, cos_t, sin_t, nh, half, tmp, rkeys, wkeys, ckeys):
    C = cos_t.unsqueeze(1).broadcast_to([128, nh, half])
    Sn = sin_t.unsqueeze(1).broadcast_to([128, nh, half])
    x1 = src[:, :, 0:half]
    x2 = src[:, :, half:2 * half]
    tk = wkeys[0] + "_tmp"
    P.op(eng, lambda e: e.tensor_tensor(out=tmp[:, 0], in0=x1, in1=C, op=ALU.mult), reads=rkeys + ckeys, writes=[tk + "0"])
    P.op(eng, lambda e: e.tensor_tensor(out=tmp[:, 1], in0=x2, in1=Sn, op=ALU.mult), reads=rkeys + ckeys, writes=[tk + "1"])
    P.op(eng, lambda e: e.tensor_tensor(out=tmp[:, 2], in0=x2, in1=C, op=ALU.mult), reads=rkeys + ckeys, writes=[tk + "2"])
    P.op(eng, lambda e: e.tensor_tensor(out=tmp[:, 3], in0=x1, in1=Sn, op=ALU.mult), reads=rkeys + ckeys, writes=[tk + "3"])
    P.op(eng, lambda e: e.tensor_tensor(out=dst[:, :, 0:half], in0=tmp[:, 0], in1=tmp[:, 1], op=ALU.subtract),
         reads=[tk + "0", tk + "1"], writes=[wkeys[0] + "_a"])
    P.op(eng, lambda e: e.tensor_tensor(out=dst[:, :, half:2 * half], in0=tmp[:, 2], in1=tmp[:, 3], op=ALU.add),
         reads=[tk + "2", tk + "3"], writes=[wkeys[0] + "_b"])
    return [wkeys[0] + "_a", wkeys[0] + "_b"]


def swa_consts():
    kk = np.arange(128)[:, None]
    qq = np.arange(128)[None, :]
    cur = np.where(kk <= qq, 0.0, -1.0)
    prev = np.where(kk > qq, 0.0, -1.0)
    m = np.stack([np.tile(prev, (1, 4)), np.tile(cur, (1, 4))], 0)
    ident = np.eye(128)
    return {"masks": _bf16(m), "ident": _bf16(ident), "bigi": _bf16(ident * BIG)}


def build_swa_prog(S=S, stage=9):
    nc = bass.Bass("TRN2", target_bir_lowering=False)
    TCH = 256
    NCH = S // TCH
    NT = S // 128
    KT = D // 128
    NW = 640
    hT = nc.dram_tensor("hT", [D, S], F32, kind="ExternalInput").ap()
    w = nc.dram_tensor("w", [D, NW], F32, kind="ExternalInput").ap()
    pos = nc.dram_tensor("pos", [128, S // 128], I32, kind="ExternalInput").ap()
    sinks = nc.dram_tensor("sinks", [1, 4], F32, kind="ExternalInput").ap()
    masks = nc.dram_tensor("masks", [2, 128, 512], BF16, kind="ExternalInput").ap()
    ident = nc.dram_tensor("ident", [128, 128], BF16, kind="ExternalInput").ap()
    bigi = nc.dram_tensor("bigi", [128, 128], BF16, kind="ExternalInput").ap()
    og = nc.dram_tensor("og", [S, 256], BF16, kind="ExternalOutput").ap()
    P = Prog(nc)
    hst = [P.sbuf(f"hst{i}", [128, KT, TCH], F32) for i in range(2)]
    hbf = [P.sbuf(f"hbf{i}", [128, KT, TCH], BF16) for i in range(2)]
    wst = P.sbuf("wst", [128, KT, NW], F32)
    wbf = P.sbuf("wbf", [128, KT, NW], BF16)
    cos_sb = P.sbuf("cos_sb", [128, NT, 8], F32)
    sin_sb = P.sbuf("sin_sb", [128, NT, 8], F32)
    mk_sb = P.sbuf("mk_sb", [128, 2, 512], BF16)
    id_sb = P.sbuf("id_sb", [128, 128], BF16)
    bi_sb = P.sbuf("bi_sb", [128, 128], BF16)
    snk = P.sbuf("snk", [128, 4], F32)
    esnk = P.sbuf("esnk", [128, 4], F32)
    vaug = P.sbuf("vaug", [128, NT, 65], BF16)
    kTl = P.sbuf("kTl", [128, NT, 128], BF16)
    kTh = P.sbuf("kTh", [128, NT, 128], BF16)
    qT = [P.sbuf(f"qT{i}", [128, 2, 128], BF16) for i in range(2)]
    qkb = [P.sbuf(f"qkb{i}", [128, 6, 64], BF16) for i in range(2)]
    rtmp = [P.sbuf(f"rtmp{i}", [128, 4, 5, 8], F32) for i in range(2)]
    qkf = [P.sbuf(f"qkf{i}", [128, 320], F32) for i in range(2)]
    zs = [P.sbuf(f"zs{i}", [128, 256], F32) for i in range(2)]
    Eb = [P.sbuf(f"Eb{i}", [128, 512], BF16) for i in range(4)]
    den = [P.sbuf(f"den{i}", [128, 4], F32) for i in range(2)]
    ob = [P.sbuf(f"ob{i}", [128, 4, 64], F32) for i in range(2)]
    osb = [P.sbuf(f"osb{i}", [128, 4, 65], F32) for i in range(2)]
    ogb = [P.sbuf(f"ogb{i}", [128, 256], BF16) for i in range(2)]
    psA = P.psum("psA", [128, 384])
    psB = P.psum("psB", [128, 256])
    psT = P.psum("psT", [128, 3, 128], BF16)
    psS = [P.psum(f"psS{i}", [128, 512]) for i in range(2)]
    psO = P.psum("psO", [128, 4, 65])

    P.dma(mk_sb[:], masks.rearrange("m p n -> p m n"), writes=["mk"])
    P.dma(id_sb[:], ident, writes=["id"])
    P.dma(bi_sb[:], bigi, writes=["bi"])
    P.dma(snk[:], sinks.partition_broadcast(128), writes=["snk"])
    P.dma(wst[:], w.rearrange("(k p) n -> p k n", p=128), writes=["wst"])
    ck, sk = emit_rotary_tables(P, pos, 8, 16, cos_sb, sin_sb, "s")
    P.op("act", lambda e: e.activation(out=esnk[:], in_=snk[:], func=AF.Exp), reads=["snk"], writes=["esnk"])
    P.op("pool", lambda e: e.tensor_copy(out=wbf[:, 0:8, :], in_=wst[:, 0:8, :]), reads=["wst"], writes=["wbf0"])
    P.op("act", lambda e: e.copy(out=wbf[:, 8:16, :], in_=wst[:, 8:16, :]), reads=["wst"], writes=["wbf1"])
    P.op("pool", lambda e: e.memset(vaug[:, :, 64:65], 1.0), writes=["vones"])
    P.op("pool", lambda e: e.memset(kTl[:], 0.0), writes=["kTz"])
    P.op("pool", lambda e: e.memset(kTh[:], 0.0), writes=["kTz2"])
    hTv = hT.rearrange("(k p) t -> p k t", p=128)

    def load_chunk(ci):
        s = ci % 2
        P.dma(hst[s][:], hTv[:, :, ci * TCH:(ci + 1) * TCH], writes=[f"hst{s}"])
        P.op("pool", lambda e: e.tensor_copy(out=hbf[s][:, 0:8, :], in_=hst[s][:, 0:8, :]),
             reads=[f"hst{s}"], writes=[f"hbf{s}a"])
        P.op("act", lambda e: e.copy(out=hbf[s][:, 8:16, :], in_=hst[s][:, 8:16, :]),
             reads=[f"hst{s}"], writes=[f"hbf{s}b"])

    load_chunk(0)
    for ci in range(NCH):
        s = ci % 2
        if ci + 1 < NCH:
            load_chunk(ci + 1)
        for j in range(TCH // 128):
            t = ci * (TCH // 128) + j
            u = t % 2
            if stage < 2:
                continue
            for k in range(KT):
                P.op("pe", lambda e, k=k, j=j, s=s: e.matmul(
                    psA[:], lhsT=hbf[s][:, k, j * 128:(j + 1) * 128], rhs=wbf[:, k, 0:384],
                    start=(k == 0), stop=(k == KT - 1)),
                    reads=[f"hbf{s}a", f"hbf{s}b", "wbf0", "wbf1"], writes=["psA"])
            for k in range(KT):
                P.op("pe", lambda e, k=k, j=j, s=s: e.matmul(
                    psB[:], lhsT=hbf[s][:, k, j * 128:(j + 1) * 128], rhs=wbf[:, k, 384:640],
                    start=(k == 0), stop=(k == KT - 1)),
                    reads=[f"hbf{s}a", f"hbf{s}b", "wbf0", "wbf1"], writes=["psB"])
            src = psA[:, 0:320].rearrange("p (h d) -> p h d", d=64)
            P.op("act", lambda e, t=t: e.copy(out=vaug[:, t, 0:64], in_=psA[:, 320:384]),
                 reads=["psA"], writes=[f"v{t}"])
            rk = []
            if stage >= 2.2:
                P.op("act", lambda e, u=u: e.copy(out=qkf[u][:], in_=psA[:, 0:320]), reads=["psA"], writes=[f"qkf{u}"])
                srcs = qkf[u][:].rearrange("p (h d) -> p h d", d=64)
                rk = emit_rotary(P, "dve", srcs, qkb[u][:, 0:5, :], cos_sb[:, t, :], sin_sb[:, t, :], 5, 8, rtmp[u],
                                 [f"qkf{u}"], [f"qkb{u}"], [ck, sk])
            if stage >= 2.4:
                P.op("pool", lambda e, u=u, srcs=srcs: e.tensor_copy(out=qkb[u][:, 0:5, 16:64], in_=srcs[:, :, 16:64]),
                     reads=[f"qkf{u}"], writes=[f"qkb{u}_c"])
            if stage >= 2.6:
                P.op("act", lambda e, u=u: e.activation(out=zs[u][:], in_=psB[:], func=AF.Silu),
                     reads=["psB"], writes=[f"zs{u}"])
            if stage >= 2.8:
                P.op("pool", lambda e, u=u: e.tensor_copy(out=qkb[u][:, 5, :], in_=qkb[u][:, 4, :]),
                     reads=rk + [f"qkb{u}_c"], writes=[f"qkb{u}_k2"])
            qkeys = rk + [f"qkb{u}_c", f"qkb{u}_k2"]
            if stage < 3:
                continue
            for g in range(3):
                P.op("pe", lambda e, g=g, u=u: e.transpose(
                    out=psT[:, g, :], in_=qkb[u][:, 2 * g:2 * g + 2, :].rearrange("p h d -> p (h d)"), identity=id_sb[:]),
                    reads=qkeys + ["id"], writes=["psT"])
            P.op("dve", lambda e, u=u: e.tensor_copy(out=qT[u][:], in_=psT[:, 0:2, :]),
                 reads=["psT"], writes=[f"qT{u}"])
            P.op("dve", lambda e, t=t: e.tensor_copy(out=kTl[0:64, t, :], in_=psT[0:64, 2, :]),
                 reads=["psT", "kTz"], writes=[f"kT{t}"])
            P.op("dve", lambda e, t=t: e.tensor_copy(out=kTh[64:128, t, :], in_=psT[64:128, 2, :]),
                 reads=["psT", "kTz2"], writes=[f"kTh{t}"])
            if stage < 4:
                continue
            kts = ([t - 1] if t > 0 else []) + [t]
            for kt in kts:
                mi = 0 if kt == t - 1 else 1
                pS = psS[mi]
                P.op("pe", lambda e, mi=mi, pS=pS: e.matmul(
                    pS[:], lhsT=bi_sb[:], rhs=mk_sb[:, mi, :], start=True, stop=False),
                    reads=["bi", "mk"], writes=[f"psS{mi}"])
                P.op("pe", lambda e, kt=kt, pS=pS, u=u: e.matmul(
                    pS[:, 0:256], lhsT=kTl[:, kt, :], rhs=qT[u][:].rearrange("p a t -> p (a t)"),
                    start=False, stop=False), reads=[f"kT{kt}", f"qT{u}"], writes=[f"psS{mi}"])
                P.op("pe", lambda e, kt=kt, pS=pS, u=u: e.matmul(
                    pS[:, 256:512], lhsT=kTh[:, kt, :], rhs=qT[u][:].rearrange("p a t -> p (a t)"),
                    start=False, stop=True), reads=[f"kTh{kt}", f"qT{u}"], writes=[f"psS{mi}"])
                eb = Eb[(t % 2) * 2 + mi]
                P.op("act", lambda e, eb=eb, pS=pS: e.activation(out=eb[:], in_=pS[:], func=AF.Exp, scale=0.125),
                     reads=[f"psS{mi}"], writes=[f"Eb{(t % 2) * 2 + mi}"])
            if stage < 5:
                continue
            colmap = [0, 2, 1, 3]
            for idx, kt in enumerate(kts):
                mi = 0 if kt == t - 1 else 1
                eb = Eb[(t % 2) * 2 + mi]
                for hh in range(4):
                    cb = colmap[hh]
                    st_ = (idx == 0 and hh == 0)
                    sp_ = (idx == len(kts) - 1 and hh == 3)
                    P.op("pe", lambda e, eb=eb, hh=hh, cb=cb, kt=kt, st_=st_, sp_=sp_: e.matmul(
                        psO[:, hh, :], lhsT=eb[:, cb * 128:(cb + 1) * 128], rhs=vaug[:, kt, :],
                        start=st_, stop=sp_),
                        reads=[f"Eb{(t % 2) * 2 + mi}", f"v{kt}", "vones"], writes=["psO"])
            if stage < 6:
                continue
            P.op("act", lambda e, u=u: e.copy(out=osb[u][:].rearrange("p h d -> p (h d)"), in_=psO[:].rearrange("p h d -> p (h d)")),
                 reads=["psO"], writes=[f"osb{u}"])
            P.op("dve", lambda e, u=u: e.tensor_tensor(out=den[u][:], in0=osb[u][:, :, 64], in1=esnk[:], op=ALU.add),
                 reads=[f"osb{u}", "esnk"], writes=[f"den{u}"])
            P.op("dve", lambda e, u=u: e.reciprocal(out=den[u][:], in_=den[u][:]), reads=[f"den{u}"], writes=[f"den{u}"])
            P.op("dve", lambda e, u=u: e.tensor_tensor(
                out=ob[u][:], in0=osb[u][:, :, 0:64], in1=den[u][:].unsqueeze(2).broadcast_to([128, 4, 64]), op=ALU.mult),
                reads=[f"osb{u}", f"den{u}"], writes=[f"ob{u}"])
            P.op("pool", lambda e, u=u: e.tensor_tensor(
                out=ogb[u][:], in0=ob[u][:].rearrange("p h d -> p (h d)"), in1=zs[u][:], op=ALU.mult),
                reads=[f"ob{u}", f"zs{u}"], writes=[f"ogb{u}"])
            P.dma(og[t * 128:(t + 1) * 128, :], ogb[u][:], reads=[f"ogb{u}"])
    P.finish()
    return nc


def run_swa(hT, w_in, sinks, positions):
    nc = _get("swa", build_swa_prog)
    cst = swa_consts()
    maps = []
    for c in range(NCORES):
        kv = c // 2
        cols = np.concatenate([
            np.arange(c * 256, (c + 1) * 256),
            2048 + np.arange(kv * 64, (kv + 1) * 64),
            2048 + 256 + np.arange(kv * 64, (kv + 1) * 64),
            2048 + 512 + np.arange(c * 256, (c + 1) * 256)])
        maps.append({"hT": hT, "w": np.ascontiguousarray(w_in[:, cols]), "pos": np.ascontiguousarray(positions.reshape(S // 128, 128).T),
                     "sinks": np.ascontiguousarray(sinks[c * 4:(c + 1) * 4]).reshape(1, 4), **cst})
    res = run_bass_kernel_spmd(nc, maps, core_ids=list(range(NCORES)))
    og = np.concatenate([r["og"] for r in res.results], axis=1)
    return np.ascontiguousarray(og.T)


def hgrn_consts():
    s_ = np.arange(128)[:, None]
    t_ = np.arange(128)[None, :]
    same = (s_ // 64) == (t_ // 64)
    m1 = (same & (s_ <= t_)).astype(np.float32)
    m2 = (same & (s_ > t_)).astype(np.float32)
    ind = np.stack([(np.arange(128) < 64), (np.arange(128) >= 64)], 1).astype(np.float32)
    return {"m1": m1, "m2": m2, "ind": ind, "ident": _bf16(np.eye(128))}


def build_hgrn_prog(S=S, use_lb=False, stage=9):
    nc = bass.Bass("TRN2", target_bir_lowering=False)
    TCH = 256
    NCH = S // TCH
    KT = D // 128
    NW = 1024
    hT = nc.dram_tensor("hT", [D, S], F32, kind="ExternalInput").ap()
    w = nc.dram_tensor("w", [D, NW], F32, kind="ExternalInput").ap()
    gn = nc.dram_tensor("gn", [1, 256], F32, kind="ExternalInput").ap()
    lbl = nc.dram_tensor("lbl", [2, 256], F32, kind="ExternalInput").ap()
    m1d = nc.dram_tensor("m1", [128, 128], F32, kind="ExternalInput").ap()
    m2d = nc.dram_tensor("m2", [128, 128], F32, kind="ExternalInput").ap()
    indd = nc.dram_tensor("ind", [128, 2], F32, kind="ExternalInput").ap()
    identd = nc.dram_tensor("ident", [128, 128], BF16, kind="ExternalInput").ap()
    og = nc.dram_tensor("og", [S, 256], BF16, kind="ExternalOutput").ap()
    P = Prog(nc)
    hst = [P.sbuf(f"hst{i}", [128, KT, TCH], F32) for i in range(2)]
    hbf = [P.sbuf(f"hbf{i}", [128, KT, TCH], BF16) for i in range(2)]
    wst = P.sbuf("wst", [128, 8, NW], F32)
    wbf = P.sbuf("wbf", [128, KT, NW], BF16)
    m1 = P.sbuf("m1s", [128, 128], F32)
    m2 = P.sbuf("m2s", [128, 128], F32)
    ind = P.sbuf("inds", [128, 2], F32)
    id_sb = P.sbuf("id_sb", [128, 128], BF16)
    gnb = P.sbuf("gnb", [128, 256], F32)
    lb0 = P.sbuf("lb0", [128, 256], F32)
    lb1 = P.sbuf("lb1", [128, 256], F32)
    lbt = P.sbuf("lbt", [128, 256], F32)
    oml = P.sbuf("oml", [128, 256], F32)
    epsb = P.sbuf("epsb", [128, 1], F32)
    Sf = [P.sbuf(f"Sf{h}", [128, 128], F32) for h in range(2)]
    Sb = [[P.sbuf(f"Sb{h}_{j}", [128, 128], BF16) for j in range(2)] for h in range(2)]
    NB = 2
    sg = [P.sbuf(f"sg{i}", [128, 256], F32) for i in range(NB)]
    lf = [P.sbuf(f"lf{i}", [128, 256], F32) for i in range(NB)]
    kk = [P.sbuf(f"kk{i}", [128, 256], F32) for i in range(NB)]
    eb = [P.sbuf(f"eb{i}", [128, 256], F32) for i in range(NB)]
    enb = [P.sbuf(f"enb{i}", [128, 256], F32) for i in range(NB)]
    ec = [P.sbuf(f"ec{i}", [128, 256], F32) for i in range(NB)]
    qt = [P.sbuf(f"qt{i}", [128, 256], BF16) for i in range(NB)]
    kt_ = [P.sbuf(f"kt{i}", [128, 256], BF16) for i in range(NB)]
    khA = [P.sbuf(f"khA{i}", [128, 256], BF16) for i in range(NB)]
    khB = [P.sbuf(f"khB{i}", [128, 256], BF16) for i in range(NB)]
    vb = [P.sbuf(f"vb{i}", [128, 256], BF16) for i in range(NB)]
    gz = [P.sbuf(f"gz{i}", [128, 256], F32) for i in range(NB)]
    dAB = [P.sbuf(f"dAB{i}", [128, 4], F32) for i in range(NB)]
    qT = [P.sbuf(f"qTf{i}", [128, 2, 128], BF16) for i in range(NB)]
    kT = [P.sbuf(f"kTf{i}", [128, 2, 128], BF16) for i in range(NB)]
    qTA = [P.sbuf(f"qTA{i}", [128, 2, 128], BF16) for i in range(NB)]
    qTB = [P.sbuf(f"qTB{i}", [128, 2, 128], BF16) for i in range(NB)]
    At = [P.sbuf(f"At{i}", [128, 2, 128], BF16) for i in range(NB)]
    osb = [P.sbuf(f"osb{i}", [128, 256], F32) for i in range(NB)]
    sq = [P.sbuf(f"sq{i}", [128, 128], F32) for i in range(NB)]
    ssq = [P.sbuf(f"ssq{i}", [128, 2], F32) for i in range(NB)]
    ogb = [P.sbuf(f"ogb{i}", [128, 256], BF16) for i in range(NB)]
    psQF = P.psum("psQF", [128, 512])
    psVZ = P.psum("psVZ", [128, 512])
    psBC = P.psum("psBC", [128, 512])
    psD = P.psum("psD", [128, 4])
    psT = P.psum("psT", [128, 4, 128], BF16)
    psA = P.psum("psA", [128, 2, 128])
    psO = P.psum("psO", [128, 2, 128])
    psU = P.psum("psU", [128, 2, 128])

    P.dma(m1[:], m1d, writes=["m1"])
    P.dma(m2[:], m2d, writes=["m2"])
    P.dma(ind[:], indd, writes=["ind"])
    P.dma(id_sb[:], identd, writes=["id"])
    P.dma(gnb[:], gn.partition_broadcast(128), writes=["gnb"])
    P.op("dve", lambda e: e.memset(epsb[:], RMS_EPS), writes=["epsb"])
    if use_lb:
        P.dma(lb0[:], lbl[0:1, :].partition_broadcast(128), writes=["lb0"])
        P.dma(lb1[:], lbl[1:2, :].partition_broadcast(128), writes=["lb1"])
        P.op("dve", lambda e: e.tensor_tensor(out=lb0[:], in0=lb1[:], in1=lb0[:], op=ALU.subtract),
             reads=["lb0", "lb1"], writes=["lb0"])
        P.op("act", lambda e: e.activation(out=lbt[:], in_=lb0[:], func=AF.Sigmoid), reads=["lb0"], writes=["lbt"])
        P.op("dve", lambda e: e.tensor_scalar(out=oml[:], in0=lbt[:], scalar1=-1.0, scalar2=1.0, op0=ALU.mult, op1=ALU.add),
             reads=["lbt"], writes=["oml"])
    wv = w.rearrange("(k p) n -> p k n", p=128)
    for half in range(2):
        P.dma(wst[:], wv[:, half * 8:(half + 1) * 8, :], writes=["wst"])
        P.op("pool", lambda e, half=half: e.tensor_copy(out=wbf[:, half * 8:half * 8 + 4, :], in_=wst[:, 0:4, :]),
             reads=["wst"], writes=[f"wbf{half}a"])
        P.op("act", lambda e, half=half: e.copy(out=wbf[:, half * 8 + 4:half * 8 + 8, :], in_=wst[:, 4:8, :]),
             reads=["wst"], writes=[f"wbf{half}b"])
    wkeys = ["wbf0a", "wbf0b", "wbf1a", "wbf1b"]
    for h in range(2):
        P.op("pool", lambda e, h=h: e.memset(Sf[h][:], 0.0), writes=[f"Sf{h}"])
        P.op("pool", lambda e, h=h: e.memset(Sb[h][0][:], 0.0), writes=[f"Sb{h}_0"])
    for i in range(NB):
        P.op("pool", lambda e, i=i: e.memset(qTA[i][:], 0.0), writes=[f"qTA{i}"])
        P.op("pool", lambda e, i=i: e.memset(qTB[i][:], 0.0), writes=[f"qTB{i}"])
    hTv = hT.rearrange("(k p) t -> p k t", p=128)

    def load_chunk(ci):
        s = ci % 2
        P.dma(hst[s][:], hTv[:, :, ci * TCH:(ci + 1) * TCH], writes=[f"hst{s}"])
        P.op("pool", lambda e: e.tensor_copy(out=hbf[s][:, 0:8, :], in_=hst[s][:, 0:8, :]),
             reads=[f"hst{s}"], writes=[f"hbf{s}a"])
        P.op("act", lambda e: e.copy(out=hbf[s][:, 8:16, :], in_=hst[s][:, 8:16, :]),
             reads=[f"hst{s}"], writes=[f"hbf{s}b"])

    load_chunk(0)
    for ci in range(NCH):
        s = ci % 2
        if ci + 1 < NCH:
            load_chunk(ci + 1)
        for j in range(TCH // 128):
            t = ci * (TCH // 128) + j
            u = t % NB
            U = str(u)
            hk = [f"hbf{s}a", f"hbf{s}b"] + wkeys
            for k in range(KT):
                P.op("pe", lambda e, k=k, j=j, s=s: e.matmul(
                    psQF[:], lhsT=hbf[s][:, k, j * 128:(j + 1) * 128], rhs=wbf[:, k, 0:512],
                    start=(k == 0), stop=(k == KT - 1)), reads=hk, writes=["psQF"])
            for k in range(KT):
                P.op("pe", lambda e, k=k, j=j, s=s: e.matmul(
                    psVZ[:], lhsT=hbf[s][:, k, j * 128:(j + 1) * 128], rhs=wbf[:, k, 512:1024],
                    start=(k == 0), stop=(k == KT - 1)), reads=hk, writes=["psVZ"])
            if stage < 2:
                continue
            P.op("act", lambda e, u=u: e.activation(out=sg[u][:], in_=psQF[:, 256:512], func=AF.Sigmoid),
                 reads=["psQF"], writes=["sg" + U])
            if use_lb:
                P.op("dve", lambda e, u=u: e.tensor_tensor(out=sg[u][:], in0=sg[u][:], in1=oml[:], op=ALU.mult),
                     reads=["sg" + U, "oml"], writes=["sg" + U])
                P.op("dve", lambda e, u=u: e.tensor_tensor(out=sg[u][:], in0=sg[u][:], in1=lbt[:], op=ALU.add),
                     reads=["sg" + U, "lbt"], writes=["sg" + U])
            P.op("act", lambda e, u=u: e.activation(out=lf[u][:], in_=sg[u][:], func=AF.Ln),
                 reads=["sg" + U], writes=["lf" + U])
            P.op("dve", lambda e, u=u: e.tensor_scalar(out=kk[u][:], in0=sg[u][:], scalar1=-1.0, scalar2=1.0,
                                                       op0=ALU.mult, op1=ALU.add), reads=["sg" + U], writes=["kk" + U])
            if stage < 3:
                continue
            P.op("pe", lambda e, u=u: e.matmul(psBC[:, 0:256], lhsT=m1[:], rhs=lf[u][:], start=True, stop=False),
                 reads=["m1", "lf" + U], writes=["psBC"])
            P.op("pe", lambda e, u=u: e.matmul(psBC[:, 256:512], lhsT=m2[:], rhs=lf[u][:], start=False, stop=True),
                 reads=["m2", "lf" + U], writes=["psBC"])
            P.op("pe", lambda e, u=u: e.matmul(psD[:, 0:2], lhsT=lf[u][:, 0:128], rhs=ind[:], start=True, stop=False),
                 reads=["ind", "lf" + U], writes=["psD"])
            P.op("pe", lambda e, u=u: e.matmul(psD[:, 2:4], lhsT=lf[u][:, 128:256], rhs=ind[:], start=False, stop=True),
                 reads=["ind", "lf" + U], writes=["psD"])
            if stage < 3.5:
                continue
            P.op("act", lambda e, u=u: e.activation(out=eb[u][:], in_=psBC[:, 0:256], func=AF.Exp),
                 reads=["psBC"], writes=["eb" + U])
            P.op("act", lambda e, u=u: e.activation(out=enb[u][:], in_=psBC[:, 0:256], func=AF.Exp, scale=-1.0),
                 reads=["psBC"], writes=["enb" + U])
            P.op("act", lambda e, u=u: e.activation(out=ec[u][:], in_=psBC[:, 256:512], func=AF.Exp),
                 reads=["psBC"], writes=["ec" + U])
            P.op("act", lambda e, u=u: e.activation(out=dAB[u][:], in_=psD[:], func=AF.Exp),
                 reads=["psD"], writes=["dAB" + U])
            P.op("act", lambda e, u=u: e.copy(out=vb[u][:], in_=psVZ[:, 0:256]), reads=["psVZ"], writes=["vb" + U])
            P.op("act", lambda e, u=u: e.activation(out=gz[u][:], in_=psVZ[:, 256:512], func=AF.Silu),
                 reads=["psVZ"], writes=["gz" + U])
            P.op("pool", lambda e, u=u: e.tensor_tensor(out=gz[u][:], in0=gz[u][:], in1=gnb[:], op=ALU.mult),
                 reads=["gz" + U, "gnb"], writes=["gz" + U])
            P.op("dve", lambda e, u=u: e.tensor_tensor(out=qt[u][:], in0=psQF[:, 0:256], in1=eb[u][:], op=ALU.mult),
                 reads=["psQF", "eb" + U], writes=["qt" + U])
            P.op("pool", lambda e, u=u: e.tensor_tensor(out=kt_[u][:], in0=kk[u][:], in1=enb[u][:], op=ALU.mult),
                 reads=["kk" + U, "enb" + U], writes=["kt" + U])
            P.op("dve", lambda e, u=u: e.scalar_tensor_tensor(out=khA[u][:], in0=kk[u][:], scalar=ind[:, 0:1], in1=ec[u][:],
                                                             op0=ALU.mult, op1=ALU.mult),
                 reads=["kk" + U, "ec" + U, "ind"], writes=["khA" + U])
            P.op("dve", lambda e, u=u: e.scalar_tensor_tensor(out=khB[u][:], in0=kk[u][:], scalar=ind[:, 1:2], in1=ec[u][:],
                                                             op0=ALU.mult, op1=ALU.mult),
                 reads=["kk" + U, "ec" + U, "ind"], writes=["khB" + U])
            if stage < 4:
                continue
            for h in range(2):
                P.op("pe", lambda e, u=u, h=h: e.transpose(out=psT[:, h, :], in_=qt[u][:, h * 128:(h + 1) * 128], identity=id_sb[:]),
                     reads=["qt" + U, "id"], writes=["psT"])
                P.op("pe", lambda e, u=u, h=h: e.transpose(out=psT[:, 2 + h, :], in_=kt_[u][:, h * 128:(h + 1) * 128], identity=id_sb[:]),
                     reads=["kt" + U, "id"], writes=["psT"])
            P.op("dve", lambda e, u=u: e.tensor_copy(out=qT[u][:].rearrange("p a t -> p (a t)"),
                                                     in_=psT[:, 0:2, :].rearrange("p a t -> p (a t)")),
                 reads=["psT"], writes=["qT" + U])
            P.op("dve", lambda e, u=u: e.tensor_copy(out=kT[u][:].rearrange("p a t -> p (a t)"),
                                                     in_=psT[:, 2:4, :].rearrange("p a t -> p (a t)")),
                 reads=["psT"], writes=["kT" + U])
            P.op("pool", lambda e, u=u: e.tensor_copy(out=qTA[u][:, :, 0:64], in_=qT[u][:, :, 0:64]),
                 reads=["qT" + U], writes=["qTA" + U])
            P.op("pool", lambda e, u=u: e.tensor_copy(out=qTB[u][:, :, 64:128], in_=qT[u][:, :, 64:128]),
                 reads=["qT" + U], writes=["qTB" + U])
            if stage < 5:
                continue
            for h in range(2):
                P.op("pe", lambda e, u=u, h=h: e.matmul(psA[:, h, :], lhsT=kT[u][:, h, :], rhs=qT[u][:, h, :],
                                                        start=True, stop=True),
                     reads=["kT" + U, "qT" + U], writes=["psA"])
                P.op("dve", lambda e, u=u, h=h: e.tensor_tensor(out=At[u][:, h, :], in0=psA[:, h, :], in1=m1[:], op=ALU.mult),
                     reads=["psA", "m1"], writes=[f"At{u}_{h}"])
            for h in range(2):
                hs = slice(h * 128, (h + 1) * 128)
                P.op("pe", lambda e, u=u, h=h, hs=hs: e.matmul(psU[:, h, :], lhsT=khA[u][:, hs], rhs=vb[u][:, hs],
                                                               start=True, stop=True),
                     reads=["khA" + U, "vb" + U], writes=["psU"])
                P.op("dve", lambda e, u=u, h=h: e.scalar_tensor_tensor(
                    out=Sf[h][:], in0=Sf[h][:], scalar=dAB[u][:, 2 * h:2 * h + 1], in1=psU[:, h, :],
                    op0=ALU.mult, op1=ALU.add), reads=[f"Sf{h}", "dAB" + U, "psU"], writes=[f"Sf{h}"])
                P.op("act", lambda e, h=h: e.copy(out=Sb[h][1][:], in_=Sf[h][:]), reads=[f"Sf{h}"], writes=[f"Sb{h}_1"])
                P.op("pe", lambda e, u=u, h=h, hs=hs: e.matmul(psO[:, h, :], lhsT=At[u][:, h, :], rhs=vb[u][:, hs],
                                                               start=True, stop=False),
                     reads=[f"At{u}_{h}", "vb" + U], writes=["psO"])
                P.op("pe", lambda e, u=u, h=h: e.matmul(psO[:, h, :], lhsT=qTA[u][:, h, :], rhs=Sb[h][0][:],
                                                        start=False, stop=False),
                     reads=["qTA" + U, f"Sb{h}_0"], writes=["psO"])
                P.op("pe", lambda e, u=u, h=h: e.matmul(psO[:, h, :], lhsT=qTB[u][:, h, :], rhs=Sb[h][1][:],
                                                        start=False, stop=True),
                     reads=["qTB" + U, f"Sb{h}_1"], writes=["psO"])
                P.op("pe", lambda e, u=u, h=h, hs=hs: e.matmul(psU[:, h, :], lhsT=khB[u][:, hs], rhs=vb[u][:, hs],
                                                               start=True, stop=True),
                     reads=["khB" + U, "vb" + U], writes=["psU"])
                P.op("dve", lambda e, u=u, h=h: e.scalar_tensor_tensor(
                    out=Sf[h][:], in0=Sf[h][:], scalar=dAB[u][:, 2 * h + 1:2 * h + 2], in1=psU[:, h, :],
                    op0=ALU.mult, op1=ALU.add), reads=[f"Sf{h}", "dAB" + U, "psU"], writes=[f"Sf{h}"])
                P.op("act", lambda e, h=h: e.copy(out=Sb[h][0][:], in_=Sf[h][:]), reads=[f"Sf{h}"], writes=[f"Sb{h}_0"])
                P.op("act", lambda e, u=u, h=h, hs=hs: e.copy(out=osb[u][:, hs], in_=psO[:, h, :]),
                     reads=["psO"], writes=[f"osb{u}_{h}"])
                P.op("dve", lambda e, u=u, h=h, hs=hs: e.tensor_tensor(out=sq[u][:], in0=osb[u][:, hs], in1=osb[u][:, hs], op=ALU.mult),
                     reads=[f"osb{u}_{h}"], writes=["sq" + U])
                P.op("dve", lambda e, u=u, h=h: e.reduce_sum(out=ssq[u][:, h:h + 1], in_=sq[u][:], axis=AX.X),
                     reads=["sq" + U], writes=[f"ssq{u}_{h}"])
                P.op("act", lambda e, u=u, h=h: e.activation(out=ssq[u][:, h:h + 1], in_=ssq[u][:, h:h + 1], func=AF.Sqrt,
                                                             bias=epsb[:, 0:1], scale=1.0 / 128.0),
                     reads=[f"ssq{u}_{h}", "epsb"], writes=[f"ssq{u}_{h}"])
                P.op("dve", lambda e, u=u, h=h: e.reciprocal(out=ssq[u][:, h:h + 1], in_=ssq[u][:, h:h + 1]),
                     reads=[f"ssq{u}_{h}"], writes=[f"ssq{u}_{h}"])
                P.op("dve", lambda e, u=u, h=h, hs=hs: e.scalar_tensor_tensor(
                    out=ogb[u][:, hs], in0=osb[u][:, hs], scalar=ssq[u][:, h:h + 1], in1=gz[u][:, hs],
                    op0=ALU.mult, op1=ALU.mult), reads=[f"osb{u}_{h}", f"ssq{u}_{h}", "gz" + U], writes=[f"ogb{u}_{h}"])
            P.dma(og[t * 128:(t + 1) * 128, :], ogb[u][:], reads=[f"ogb{u}_0", f"ogb{u}_1"])
    P.finish()
    return nc


def run_hgrn(hT, w_in, g_norm, lb_logits, use_lb):
    nc = _get("hgrn%d" % int(use_lb), lambda: build_hgrn_prog(use_lb=use_lb))
    cst = hgrn_consts()
    maps = []
    for c in range(NCORES):
        cs = np.arange(c * 256, (c + 1) * 256)
        cols = np.concatenate([cs, 2048 + cs, 4096 + cs, 6144 + cs])
        maps.append({"hT": hT, "w": np.ascontiguousarray(w_in[:, cols]),
                     "gn": np.ascontiguousarray(g_norm[cs]).reshape(1, 256),
                     "lbl": np.ascontiguousarray(lb_logits[:, cs]), **cst})
    res = run_bass_kernel_spmd(nc, maps, core_ids=list(range(NCORES)))
    og = np.concatenate([r["og"] for r in res.results], axis=1)
    return np.ascontiguousarray(og.T)


NSA_IN = 7216


def build_nsa_proj_prog(ntok=1024):
    nc = bass.Bass("TRN2", target_bir_lowering=False)
    KT = D // 128
    NT = ntok // 128
    hT = nc.dram_tensor("hT", [D, ntok], F32, kind="ExternalInput").ap()
    w = nc.dram_tensor("w", [D, NSA_IN], F32, kind="ExternalInput").ap()
    pos = nc.dram_tensor("pos", [128, NT], I32, kind="ExternalInput").ap()
    oa = nc.dram_tensor("oa", [ntok, 5120], BF16, kind="ExternalOutput").ap()
    ob = nc.dram_tensor("ob", [ntok, 2096], F32, kind="ExternalOutput").ap()
    P = Prog(nc)
    hst = P.sbuf("hst", [128, 8, ntok], F32)
    hbf = P.sbuf("hbf", [128, KT, ntok], BF16)
    wst = P.sbuf("wst", [128, KT, 512], F32)
    wbf = [P.sbuf(f"wbf{i}", [128, KT, 512], BF16) for i in range(2)]
    cos_sb = P.sbuf("cos_sb", [128, NT, 16], F32)
    sin_sb = P.sbuf("sin_sb", [128, NT, 16], F32)
    ev = [P.sbuf(f"ev{i}", [128, 512], F32) for i in range(2)]
    o16 = [P.sbuf(f"o16{i}", [128, 512], BF16) for i in range(2)]
    rtmp = [P.sbuf(f"rtmp{i}", [128, 4, 4, 16], F32) for i in range(2)]
    ps = [P.psum(f"ps{i}", [128, 512]) for i in range(2)]
    hTv = hT.rearrange("(k p) t -> p k t", p=128)
    for half in range(2):
        P.dma(hst[:], hTv[:, half * 8:(half + 1) * 8, :], writes=["hst"])
        P.op("pool", lambda e, half=half: e.tensor_copy(out=hbf[:, half * 8:half * 8 + 4, :], in_=hst[:, 0:4, :]),
             reads=["hst"], writes=[f"hbf{half}a"])
        P.op("act", lambda e, half=half: e.copy(out=hbf[:, half * 8 + 4:half * 8 + 8, :], in_=hst[:, 4:8, :]),
             reads=["hst"], writes=[f"hbf{half}b"])
    hkeys = ["hbf0a", "hbf0b", "hbf1a", "hbf1b"]
    ck, sk = emit_rotary_tables(P, pos, 16, 32, cos_sb, sin_sb, "n")
    chunks = [(i * 512, 512, "rot") for i in range(4)]
    chunks += [(2048, 512, "rot"), (2560, 512, "plain"), (3072, 512, "rot"), (3584, 512, "plain"),
               (4096, 512, "rot"), (4608, 512, "plain"), (5120, 48, "f32")]
    chunks += [(5168 + i * 512, 512, "f32") for i in range(4)]
    wv = w.rearrange("(k p) n -> p k n", p=128)
    it = 0
    for ci, (c0, ncol, kind) in enumerate(chunks):
        wb = ci % 2
        P.dma(wst[:, :, 0:ncol], wv[:, :, c0:c0 + ncol], writes=["wst"])
        P.op("pool", lambda e, wb=wb, ncol=ncol: e.tensor_copy(out=wbf[wb][:, 0:8, 0:ncol], in_=wst[:, 0:8, 0:ncol]),
             reads=["wst"], writes=[f"wbf{wb}a"])
        P.op("act", lambda e, wb=wb, ncol=ncol: e.copy(out=wbf[wb][:, 8:16, 0:ncol], in_=wst[:, 8:16, 0:ncol]),
             reads=["wst"], writes=[f"wbf{wb}b"])
        for t in range(NT):
            u = it % 2
            it += 1
            U = str(u)
            for k in range(KT):
                P.op("pe", lambda e, k=k, t=t, u=u, wb=wb, ncol=ncol: e.matmul(
                    ps[u][:, 0:ncol], lhsT=hbf[:, k, t * 128:(t + 1) * 128], rhs=wbf[wb][:, k, 0:ncol],
                    start=(k == 0), stop=(k == KT - 1)),
                    reads=hkeys + [f"wbf{wb}a", f"wbf{wb}b"], writes=["ps" + U])
            rows = slice(t * 128, (t + 1) * 128)
            if kind == "f32":
                P.op("act", lambda e, u=u, ncol=ncol: e.copy(out=ev[u][:, 0:ncol], in_=ps[u][:, 0:ncol]),
                     reads=["ps" + U], writes=["ev" + U])
                P.dma(ob[rows, c0 - 5120:c0 - 5120 + ncol], ev[u][:, 0:ncol], reads=["ev" + U])
            elif kind == "plain":
                P.op("act", lambda e, u=u: e.copy(out=o16[u][:], in_=ps[u][:]), reads=["ps" + U], writes=["o16" + U])
                P.dma(oa[rows, c0:c0 + 512], o16[u][:], reads=["o16" + U])
            else:
                P.op("act", lambda e, u=u: e.copy(out=ev[u][:], in_=ps[u][:]), reads=["ps" + U], writes=["ev" + U])
                src = ev[u][:].rearrange("p (h d) -> p h d", d=128)
                dst = o16[u][:].rearrange("p (h d) -> p h d", d=128)
                rk = emit_rotary(P, "dve", src, dst, cos_sb[:, t, :], sin_sb[:, t, :], 4, 16, rtmp[u],
                                 ["ev" + U], ["o16" + U], [ck, sk])
                P.op("pool", lambda e, src=src, dst=dst: e.tensor_copy(out=dst[:, :, 32:128], in_=src[:, :, 32:128]),
                     reads=["ev" + U], writes=["o16" + U + "_c"])
                P.dma(oa[rows, c0:c0 + 512], o16[u][:], reads=rk + ["o16" + U + "_c"])
    P.finish()
    return nc


def run_nsa_proj(hT, w_in, positions):
    nc = _get("nsa_proj", build_nsa_proj_prog)
    maps = []
    for c in range(NCORES):
        ts = slice(c * 1024, (c + 1) * 1024)
        maps.append({"hT": np.ascontiguousarray(hT[:, ts]), "w": w_in,
                     "pos": np.ascontiguousarray(positions.reshape(-1)[ts].reshape(8, 128).T)})
    res = run_bass_kernel_spmd(nc, maps, core_ids=list(range(NCORES)))
    oa = np.concatenate([r["oa"] for r in res.results], axis=0)
    ob = np.concatenate([r["ob"] for r in res.results], axis=0)
    return oa, ob


def build_nsa_cmp_prog():
    nc = bass.Bass("TRN2", target_bir_lowering=False)
    x = nc.dram_tensor("x", [128, 16, 512], BF16, kind="ExternalInput").ap()
    w1 = nc.dram_tensor("w1", [4096, 256], F32, kind="ExternalInput").ap()
    w2 = nc.dram_tensor("w2", [256, 128], F32, kind="ExternalInput").ap()
    posT = nc.dram_tensor("posT", [128, 32], F32, kind="ExternalInput").ap()
    out = nc.dram_tensor("out", [512, 128], F32, kind="ExternalOutput").ap()
    P = Prog(nc)
    x_sb = P.sbuf("x_sb", [128, 16, 512], BF16)
    w1s = P.sbuf("w1s", [128, 32, 256], F32)
    w1b = P.sbuf("w1b", [128, 32, 256], BF16)
    w2s = P.sbuf("w2s", [128, 2, 128], F32)
    w2b = P.sbuf("w2b", [128, 2, 128], BF16)
    pts = P.sbuf("pts", [128, 32], F32)
    ptb = P.sbuf("ptb", [128, 32], BF16)
    bias = P.sbuf("bias", [128, 2], F32)
    hid = P.sbuf("hid", [128, 2, 512], BF16)
    osb = [P.sbuf(f"osb{i}", [128, 128], F32) for i in range(2)]
    psH = [P.psum(f"psH{i}", [128, 512]) for i in range(2)]
    psB = P.psum("psB", [128, 2])
    psO = [P.psum(f"psO{i}", [128, 128]) for i in range(2)]
    P.dma(x_sb[:], x, writes=["x"])
    P.dma(w1s[:], w1.rearrange("(j p) n -> p j n", p=128), writes=["w1s"])
    P.dma(w2s[:], w2.rearrange("(k p) n -> p k n", p=128), writes=["w2s"])
    P.dma(pts[:], posT, writes=["pts"])
    P.op("pool", lambda e: e.tensor_copy(out=w1b[:, 0:16, :], in_=w1s[:, 0:16, :]), reads=["w1s"], writes=["w1ba"])
    P.op("act", lambda e: e.copy(out=w1b[:, 16:32, :], in_=w1s[:, 16:32, :]), reads=["w1s"], writes=["w1bb"])
    P.op("dve", lambda e: e.tensor_copy(out=w2b[:], in_=w2s[:]), reads=["w2s"], writes=["w2b"])
    P.op("dve", lambda e: e.tensor_copy(out=ptb[:], in_=pts[:]), reads=["pts"], writes=["ptb"])
    P.op("pool", lambda e: e.memset(hid[:], 0.0), writes=["hid0", "hid1"])
    wk = ["w1ba", "w1bb"]
    for hf in range(2):
        for j in range(32):
            P.op("pe", lambda e, hf=hf, j=j: e.matmul(psB[:, hf:hf + 1], lhsT=w1b[:, j, hf * 128:(hf + 1) * 128],
                                                      rhs=ptb[:, j:j + 1], start=(j == 0 and hf == 0), stop=(j == 31 and hf == 1)),
                 reads=wk + ["ptb"], writes=["psB"])
    P.op("dve", lambda e: e.tensor_copy(out=bias[:], in_=psB[:]), reads=["psB"], writes=["bias"])
    for hf in range(2):
        for j in range(32):
            P.op("pe", lambda e, hf=hf, j=j: e.matmul(
                psH[hf][:, 0:511], lhsT=w1b[:, j, hf * 128:(hf + 1) * 128],
                rhs=x_sb[:, j % 16, (j // 16):(j // 16) + 511], start=(j == 0), stop=(j == 31)),
                reads=wk + ["x"], writes=[f"psH{hf}"])
        P.op("act", lambda e, hf=hf: e.activation(out=hid[:, hf, 0:511], in_=psH[hf][:, 0:511], func=AF.Silu,
                                                  bias=bias[:, hf:hf + 1], scale=1.0),
             reads=[f"psH{hf}", "bias", f"hid{hf}"], writes=[f"hid{hf}"])
    for it in range(4):
        u = it % 2
        for hf in range(2):
            P.op("pe", lambda e, it=it, hf=hf, u=u: e.matmul(psO[u][:], lhsT=hid[:, hf, it * 128:(it + 1) * 128],
                                                            rhs=w2b[:, hf, :], start=(hf == 0), stop=(hf == 1)),
                 reads=["hid0", "hid1", "w2b"], writes=[f"psO{u}"])
        P.op("act", lambda e, u=u: e.copy(out=osb[u][:], in_=psO[u][:]), reads=[f"psO{u}"], writes=[f"osb{u}"])
        P.dma(out[it * 128:(it + 1) * 128, :], osb[u][:], reads=[f"osb{u}"])
    P.finish()
    return nc


def run_nsa_cmp(kc, vc, posk, w1k, w2k, posv, w1v, w2v):
    nc = _get("nsa_cmp", build_nsa_cmp_prog)
    maps = []
    for c in range(NCORES):
        g, isv = c // 2, c % 2
        src = vc if isv else kc
        xg = src[:, g, :]
        xr = np.ascontiguousarray(xg.reshape(512, 16, 128).transpose(2, 1, 0))
        maps.append({"x": xr, "w1": w1v if isv else w1k, "w2": w2v if isv else w2k,
                     "posT": np.ascontiguousarray((posv if isv else posk).T)})
    res = run_bass_kernel_spmd(nc, maps, core_ids=list(range(NCORES)))
    kcb = np.stack([res.results[2 * g]["out"] for g in range(4)], 0)
    vcb = np.stack([res.results[2 * g + 1]["out"] for g in range(4)], 0)
    return kcb, vcb


NSA_SCALE = 128.0 ** -0.5
NEG = -1.0e30


def nsa_core_inputs(oa, ob, kcb, vcb, g, pair, chunks, cst, bonus):
    tok = np.concatenate([np.arange(c * 512, (c + 1) * 512) for c in chunks])
    horder = [2 * pair, 2 * pair + 1, 2 * (1 - pair), 2 * (1 - pair) + 1]
    q = oa[tok][:, g * 512:(g + 1) * 512].reshape(len(tok), 4, 128)[:, horder, :]
    gcols = np.array([b * 16 + 4 * g + hh for b in range(3) for hh in horder])
    zc = 48 + (4 * g + 2 * pair) * 128
    m = {"qT": np.ascontiguousarray(q.transpose(2, 1, 0)),
         "kTs": np.ascontiguousarray(oa[:, 3072 + g * 128:3072 + (g + 1) * 128].T),
         "vs": np.ascontiguousarray(oa[:, 3584 + g * 128:3584 + (g + 1) * 128]),
         "kTw": np.ascontiguousarray(oa[:, 4096 + g * 128:4096 + (g + 1) * 128].T),
         "vw": np.ascontiguousarray(oa[:, 4608 + g * 128:4608 + (g + 1) * 128]),
         "kcT": np.ascontiguousarray(kcb[g].T), "vc": np.ascontiguousarray(vcb[g]),
         "glog": np.ascontiguousarray(ob[tok][:, gcols]),
         "z": np.ascontiguousarray(ob[tok][:, zc:zc + 256]),
         "bonus": np.ascontiguousarray(bonus.reshape(16, 4, 128, 128)[list(chunks)].reshape(-1, 128, 128))}
    m.update(cst)
    return m


def run_nsa_attn(oa, ob, kcb, vcb):
    cst, bonus = nsa_consts()
    chunks = list(range(16))
    nc = _get("nsa_attn", lambda: build_nsa_attn_prog(chunks))
    maps = [nsa_core_inputs(oa, ob, kcb, vcb, c // 2, c % 2, chunks, cst, bonus) for c in range(NCORES)]
    res = run_bass_kernel_spmd(nc, maps, core_ids=list(range(NCORES)))
    og = np.concatenate([r["og"] for r in res.results], axis=1)
    return np.ascontiguousarray(og.T)


def nsa_consts():
    kk = np.arange(128)[:, None]
    qq = np.arange(512)[None, :]
    winm = np.zeros((8, 128, 512), np.float32)
    for oi in range(8):
        kp = 128 * (oi - 4) + kk
        dd = qq - kp
        winm[oi] = np.where((dd >= 0) & (dd < 512), 0.0, -1.0)
    cmpm = np.zeros((5, 128, 512), np.float32)
    for dl in range(5):
        cmpm[dl] = np.where(qq >= 16 * kk + 31 - 512 * dl, 0.0, -1.0)
    jj = np.arange(128)[:, None]
    ex = np.zeros((128, 64, 128), np.float32)
    for kt in range(64):
        ex[:, kt, :] = np.where(jj == 2 * kt + np.arange(128)[None, :] // 64, BIG, 0.0)
    ii = np.arange(512)[:, None]
    j2 = np.arange(128)[None, :]
    agg = ((ii >= 4 * j2 - 1) & (ii <= 4 * j2 + 3) & (ii < 511)).astype(np.float32)
    t = np.arange(S)[:, None]
    cur = t // 64
    allowed = j2 * 64 <= t
    forced = (j2 == 0) | (j2 == cur) | (j2 == cur - 1)
    bonus = np.where(allowed, np.where(forced, 1.0e4, 0.0), NEG).astype(np.float32).reshape(64, 128, 128)
    ident = np.eye(128)
    return {"winm": _bf16(winm), "cmpm": _bf16(cmpm), "ex": _bf16(ex), "agg": _bf16(agg),
            "ident": _bf16(ident), "bigi": _bf16(ident * BIG)}, bonus


def build_nsa_attn_prog(chunks, debug=False):
    nc = bass.Bass("TRN2", target_bir_lowering=False)
    nchunks = len(chunks)
    NQ = nchunks * 512
    qT = nc.dram_tensor("qT", [128, 4, NQ], BF16, kind="ExternalInput").ap()
    kTs_d = nc.dram_tensor("kTs", [128, S], BF16, kind="ExternalInput").ap()
    kTw_d = nc.dram_tensor("kTw", [128, S], BF16, kind="ExternalInput").ap()
    vs_d = nc.dram_tensor("vs", [S, 128], BF16, kind="ExternalInput").ap()
    vw_d = nc.dram_tensor("vw", [S, 128], BF16, kind="ExternalInput").ap()
    kcT_d = nc.dram_tensor("kcT", [128, 512], F32, kind="ExternalInput").ap()
    vc_d = nc.dram_tensor("vc", [512, 128], F32, kind="ExternalInput").ap()
    glog_d = nc.dram_tensor("glog", [NQ, 12], F32, kind="ExternalInput").ap()
    z_d = nc.dram_tensor("z", [NQ, 256], F32, kind="ExternalInput").ap()
    bonus_d = nc.dram_tensor("bonus", [NQ // 128, 128, 128], F32, kind="ExternalInput").ap()
    winm_d = nc.dram_tensor("winm", [8, 128, 512], BF16, kind="ExternalInput").ap()
    cmpm_d = nc.dram_tensor("cmpm", [5, 128, 512], BF16, kind="ExternalInput").ap()
    ex_d = nc.dram_tensor("ex", [128, 64, 128], BF16, kind="ExternalInput").ap()
    agg_d = nc.dram_tensor("agg", [512, 128], BF16, kind="ExternalInput").ap()
    ident_d = nc.dram_tensor("ident", [128, 128], BF16, kind="ExternalInput").ap()
    bigi_d = nc.dram_tensor("bigi", [128, 128], BF16, kind="ExternalInput").ap()
    og = nc.dram_tensor("og", [NQ, 256], BF16, kind="ExternalOutput").ap()
    if debug:
        dbg_sel = nc.dram_tensor("dbg_sel", [128, 512], BF16, kind="ExternalOutput").ap()
        dbg_imp = nc.dram_tensor("dbg_imp", [128, 4, 128], F32, kind="ExternalOutput").ap()
    P = Prog(nc)
    kTs = P.sbuf("kTs_sb", [128, S], BF16)
    kTw = P.sbuf("kTw_sb", [128, S], BF16)
    vsa = P.sbuf("vsa", [128, 64, 129], BF16)
    vwa = P.sbuf("vwa", [128, 64, 129], BF16)
    kcTf = P.sbuf("kcTf", [128, 512], F32)
    kcT = P.sbuf("kcT_sb", [128, 512], BF16)
    vcf = P.sbuf("vcf", [128, 4, 128], F32)
    vca = P.sbuf("vca", [128, 4, 257], BF16)
    winm = P.sbuf("winm_sb", [128, 8, 512], BF16)
    cmpm = P.sbuf("cmpm_sb", [128, 5, 512], BF16)
    ex = P.sbuf("ex_sb", [128, 64, 128], BF16)
    id_sb = P.sbuf("id_sb", [128, 128], BF16)
    bi_sb = P.sbuf("bi_sb", [128, 128], BF16)
    qc = [P.sbuf(f"qc{i}", [128, 4, 512], BF16) for i in range(2)]
    gl = P.sbuf("gl", [128, 4, 12], F32)
    gs = P.sbuf("gs", [128, 4, 12], F32)
    zf = P.sbuf("zf", [128, 4, 256], F32)
    zs = P.sbuf("zs", [128, 4, 256], F32)
    bon = P.sbuf("bon", [128, 4, 128], F32)
    Ec = [P.sbuf(f"Ec{i}", [128, 512], BF16) for i in range(4)]
    Eb = [P.sbuf(f"Eb{i}", [128, 512], BF16) for i in range(3)]
    csb = [P.sbuf(f"csb{i}", [128, 257], F32) for i in range(2)]
    vsb = [P.sbuf(f"vsb{i}", [128, 2, 129], F32) for i in range(2)]
    rc = [P.sbuf(f"rc{i}", [128, 1], F32) for i in range(4)]
    rg = [P.sbuf(f"rg{i}", [128, 1], F32) for i in range(4)]
    acc = P.sbuf("acc", [128, 4, 256], F32)
    imp = P.sbuf("imp", [128, 4, 128], F32)
    impb = P.sbuf("impb", [128, 128], F32)
    wrk = P.sbuf("wrk", [128, 128], F32)
    m8a = P.sbuf("m8a", [128, 8], F32)
    m8b = P.sbuf("m8b", [128, 8], F32)
    self_ = P.sbuf("self", [128, 128], F32)
    selm = P.sbuf("selm", [128, 128], BF16)
    selT = P.sbuf("selT", [128, 512], BF16)
    ogb = [P.sbuf(f"ogb{i}", [128, 256], BF16) for i in range(2)]
    psS = [P.psum(f"psS{i}", [128, 512]) for i in range(2)]
    psC = P.psum("psC", [128, 257])
    psV = [P.psum(f"psV{i}", [128, 2, 129]) for i in range(2)]
    psT = P.psum("psT", [128, 128], BF16)

    for q4 in range(4):
        cs = slice(q4 * 2048, (q4 + 1) * 2048)
        P.dma(kTs[:, cs], kTs_d[:, cs], writes=[f"kTs{q4}"])
        P.dma(kTw[:, cs], kTw_d[:, cs], writes=[f"kTw{q4}"])
    kTs_keys = [f"kTs{i}" for i in range(4)]
    kTw_keys = [f"kTw{i}" for i in range(4)]
    vs_v = vs_d.rearrange("(t p) d -> p t d", p=128)
    vw_v = vw_d.rearrange("(t p) d -> p t d", p=128)
    for q4 in range(4):
        P.dma(vsa[:, q4 * 16:(q4 + 1) * 16, 0:128], vs_v[:, q4 * 16:(q4 + 1) * 16, :], writes=[f"vsa_{q4}"])
        P.dma(vwa[:, q4 * 16:(q4 + 1) * 16, 0:128], vw_v[:, q4 * 16:(q4 + 1) * 16, :], writes=[f"vwa_{q4}"])
    P.op("pool", lambda e: e.memset(vsa[:, :, 128:129], 1.0), writes=["vsa1"])
    P.op("pool", lambda e: e.memset(vwa[:, :, 128:129], 1.0), writes=["vwa1"])
    P.dma(kcTf[:], kcT_d, writes=["kcTf"])
    P.op("dve", lambda e: e.tensor_copy(out=kcT[:], in_=kcTf[:]), reads=["kcTf"], writes=["kcT"])
    P.dma(vcf[:], vc_d.rearrange("(t p) d -> p t d", p=128), writes=["vcf"])
    P.op("dve", lambda e: e.tensor_copy(out=vca[:, :, 0:128], in_=vcf[:]), reads=["vcf"], writes=["vca0"])
    P.dma(vca[:, :, 128:256], agg_d.rearrange("(t p) j -> p t j", p=128), writes=["vca1"])
    P.op("pool", lambda e: e.memset(vca[:, :, 256:257], 1.0), writes=["vca2"])
    vca_keys = ["vca0", "vca1", "vca2"]
    P.dma(winm[:], winm_d.rearrange("m p n -> p m n"), writes=["winm"])
    P.dma(cmpm[:], cmpm_d.rearrange("m p n -> p m n"), writes=["cmpm"])
    P.dma(ex[:], ex_d, writes=["ex"])
    P.dma(id_sb[:], ident_d, writes=["id"])
    P.dma(bi_sb[:], bigi_d, writes=["bi"])

    def load_q(i):
        P.dma(qc[i % 2][:], qT[:, :, i * 512:(i + 1) * 512], writes=[f"qc{i % 2}"])

    sidx = [0]

    def score_tile(masks, kT_ap, q_ap, kkeys, qkey):
        b = sidx[0] % 2
        sidx[0] += 1
        first = True
        for (ml, mr, mkeys) in masks:
            P.op("pe", lambda e, ml=ml, mr=mr, b=b, first=first: e.matmul(psS[b][:], lhsT=ml, rhs=mr, start=first, stop=False),
                 reads=mkeys, writes=[f"psS{b}"])
            first = False
        P.op("pe", lambda e, b=b, first=first, kT_ap=kT_ap, q_ap=q_ap: e.matmul(psS[b][:], lhsT=kT_ap, rhs=q_ap, start=first, stop=True),
             reads=kkeys + [qkey], writes=[f"psS{b}"])
        return b

    load_q(0)
    eidx = 0
    for i in range(nchunks):
        c = chunks[i]
        u = i % 2
        qk = f"qc{u}"
        if i + 1 < nchunks:
            load_q(i + 1)
        rows = slice(i * 512, (i + 1) * 512)
        P.dma(gl[:], glog_d[rows, :].rearrange("(t p) c -> p t c", p=128), writes=["gl"])
        P.op("act", lambda e: e.activation(out=gs[:], in_=gl[:], func=AF.Sigmoid), reads=["gl"], writes=["gs"])
        P.dma(zf[:], z_d[rows, :].rearrange("(t p) c -> p t c", p=128), writes=["zf"])
        P.op("act", lambda e: e.activation(out=zs[:], in_=zf[:], func=AF.Silu), reads=["zf"], writes=["zs"])
        P.dma(bon[:], bonus_d[4 * i:4 * i + 4].rearrange("t p j -> p t j"), writes=["bon"])
        ntn = min(3, c // 4) + 1
        for hh in range(4):
            for tn in range(ntn):
                dl = c - 4 * tn
                masks = [(bi_sb[:], cmpm[:, dl, :], ["bi", "cmpm"])] if dl <= 4 else []
                b = score_tile(masks, kcT[:, tn * 128:(tn + 1) * 128], qc[u][:, hh, :], ["kcT"], qk)
                P.op("act", lambda e, b=b, tn=tn: e.activation(out=Ec[tn][:], in_=psS[b][:], func=AF.Exp, scale=NSA_SCALE),
                     reads=[f"psS{b}"], writes=[f"Ec{tn}"])
            for qt in range(4):
                cb = (hh * 4 + qt) % 2
                r4 = qt
                for tn in range(ntn):
                    P.op("pe", lambda e, tn=tn, qt=qt, s0=(tn == 0), s1=(tn == ntn - 1): e.matmul(
                        psC[:], lhsT=Ec[tn][:, qt * 128:(qt + 1) * 128], rhs=vca[:, tn, :], start=s0, stop=s1),
                        reads=[f"Ec{tn}"] + vca_keys, writes=["psC"])
                P.op("act", lambda e, cb=cb: e.copy(out=csb[cb][:], in_=psC[:]), reads=["psC"], writes=[f"csb{cb}"])
                P.op("dve", lambda e, cb=cb, r4=r4: e.tensor_scalar(out=rc[r4][:], in0=csb[cb][:, 256:257], scalar1=1.0e-30,
                                                                    scalar2=None, op0=ALU.max),
                     reads=[f"csb{cb}"], writes=[f"rc{r4}"])
                P.op("dve", lambda e, r4=r4: e.reciprocal(out=rc[r4][:], in_=rc[r4][:]), reads=[f"rc{r4}"], writes=[f"rc{r4}"])
                if hh < 2:
                    P.op("dve", lambda e, r4=r4, qt=qt, hh=hh: e.tensor_tensor(out=rg[r4][:], in0=rc[r4][:], in1=gs[:, qt, hh:hh + 1],
                                                                               op=ALU.mult),
                         reads=[f"rc{r4}", "gs"], writes=[f"rg{r4}"])
                    P.op("dve", lambda e, cb=cb, r4=r4, qt=qt, hh=hh: e.tensor_scalar(
                        out=acc[:, qt, hh * 128:(hh + 1) * 128], in0=csb[cb][:, 0:128], scalar1=rg[r4][:, 0:1], scalar2=None,
                        op0=ALU.mult), reads=[f"csb{cb}", f"rg{r4}"], writes=[f"acc{qt}_{hh}"])
                if hh == 0:
                    P.op("dve", lambda e, cb=cb, r4=r4, qt=qt: e.tensor_scalar(
                        out=imp[:, qt, :], in0=csb[cb][:, 128:256], scalar1=rc[r4][:, 0:1], scalar2=None, op0=ALU.mult),
                        reads=[f"csb{cb}", f"rc{r4}"], writes=[f"imp{qt}"])
                else:
                    P.op("dve", lambda e, cb=cb, r4=r4, qt=qt: e.scalar_tensor_tensor(
                        out=imp[:, qt, :], in0=csb[cb][:, 128:256], scalar=rc[r4][:, 0:1], in1=imp[:, qt, :],
                        op0=ALU.mult, op1=ALU.add), reads=[f"csb{cb}", f"rc{r4}", f"imp{qt}"], writes=[f"imp{qt}"])
        for qt in range(4):
            P.op("dve", lambda e, qt=qt: e.tensor_tensor(out=impb[:], in0=imp[:, qt, :], in1=bon[:, qt, :], op=ALU.add),
                 reads=[f"imp{qt}", "bon"], writes=["impb"])
            P.op("dve", lambda e: e.max(out=m8a[:], in_=impb[:]), reads=["impb"], writes=["m8a"])
            P.op("dve", lambda e: e.match_replace(out=wrk[:], in_to_replace=m8a[:], in_values=impb[:], imm_value=NEG),
                 reads=["impb", "m8a"], writes=["wrk"])
            P.op("dve", lambda e: e.max(out=m8b[:], in_=wrk[:]), reads=["wrk"], writes=["m8b"])
            P.op("dve", lambda e: e.tensor_scalar(out=self_[:], in0=impb[:], scalar1=m8b[:, 7:8], scalar2=-1.0,
                                                  op0=ALU.is_ge, op1=ALU.add), reads=["impb", "m8b"], writes=["self"])
            P.op("dve", lambda e: e.tensor_copy(out=selm[:], in_=self_[:]), reads=["self"], writes=["selm"])
            P.op("pe", lambda e: e.transpose(out=psT[:], in_=selm[:], identity=id_sb[:]), reads=["selm", "id"], writes=["psT"])
            P.op("dve", lambda e, qt=qt: e.tensor_copy(out=selT[:, qt * 128:(qt + 1) * 128], in_=psT[:]),
                 reads=["psT"], writes=["selT"])
        if debug:
            P.dma(dbg_sel, selT[:], reads=["selT"])
            P.dma(dbg_imp, imp[:], reads=[f"imp{qt}" for qt in range(4)])
        for br in (1, 2):
            if br == 1:
                kts = list(range(4 * c + 4))
            else:
                kts = list(range(max(0, 4 * c - 4), 4 * c + 4))
            kT_sb, vaug = (kTs, vsa) if br == 1 else (kTw, vwa)
            kkeys = kTs_keys if br == 1 else kTw_keys
            vkeys = ([f"vsa_{i}" for i in range(4)] + ["vsa1"]) if br == 1 else ([f"vwa_{i}" for i in range(4)] + ["vwa1"])
            for hh in range(2):
                for ki, kt in enumerate(kts):
                    masks = []
                    if br == 1:
                        masks.append((ex[:, kt, :], selT[:], ["ex", "selT"]))
                    if kt >= 4 * c - 4 and (br == 2 or kt >= 4 * c):
                        masks.append((bi_sb[:], winm[:, kt - 4 * c + 4, :], ["bi", "winm"]))
                    b = score_tile(masks, kT_sb[:, kt * 128:(kt + 1) * 128], qc[u][:, hh, :], kkeys, qk)
                    eb = eidx % 3
                    eidx += 1
                    P.op("act", lambda e, b=b, eb=eb: e.activation(out=Eb[eb][:], in_=psS[b][:], func=AF.Exp, scale=NSA_SCALE),
                         reads=[f"psS{b}"], writes=[f"Eb{eb}"])
                    for qt in range(4):
                        bk, q2 = qt // 2, qt % 2
                        st_ = (ki == 0 and q2 == 0)
                        sp_ = (ki == len(kts) - 1 and q2 == 1)
                        P.op("pe", lambda e, eb=eb, qt=qt, bk=bk, q2=q2, kt=kt, st_=st_, sp_=sp_, vaug=vaug: e.matmul(
                            psV[bk][:, q2, :], lhsT=Eb[eb][:, qt * 128:(qt + 1) * 128], rhs=vaug[:, kt, :], start=st_, stop=sp_),
                            reads=[f"Eb{eb}"] + vkeys, writes=[f"psV{bk}"])
                for bk in range(2):
                    P.op("act", lambda e, bk=bk: e.copy(out=vsb[bk][:].rearrange("p a d -> p (a d)"),
                                                        in_=psV[bk][:].rearrange("p a d -> p (a d)")),
                         reads=[f"psV{bk}"], writes=[f"vsb{bk}"])
                    for q2 in range(2):
                        qt = bk * 2 + q2
                        r4 = qt
                        P.op("dve", lambda e, bk=bk, q2=q2, r4=r4: e.reciprocal(out=rc[r4][:], in_=vsb[bk][:, q2, 128:129]),
                             reads=[f"vsb{bk}"], writes=[f"rc{r4}"])
                        P.op("dve", lambda e, r4=r4, qt=qt, hh=hh, br=br: e.tensor_tensor(
                            out=rg[r4][:], in0=rc[r4][:], in1=gs[:, qt, br * 4 + hh:br * 4 + hh + 1], op=ALU.mult),
                            reads=[f"rc{r4}", "gs"], writes=[f"rg{r4}"])
                        P.op("dve", lambda e, bk=bk, q2=q2, r4=r4, qt=qt, hh=hh: e.scalar_tensor_tensor(
                            out=acc[:, qt, hh * 128:(hh + 1) * 128], in0=vsb[bk][:, q2, 0:128], scalar=rg[r4][:, 0:1],
                            in1=acc[:, qt, hh * 128:(hh + 1) * 128], op0=ALU.mult, op1=ALU.add),
                            reads=[f"vsb{bk}", f"rg{r4}", f"acc{qt}_{hh}"], writes=[f"acc{qt}_{hh}"])
        for qt in range(4):
            ob_ = qt % 2
            P.op("pool", lambda e, qt=qt, ob_=ob_: e.tensor_tensor(out=ogb[ob_][:], in0=acc[:, qt, :], in1=zs[:, qt, :], op=ALU.mult),
                 reads=[f"acc{qt}_{hh}" for hh in range(2)] + ["zs"], writes=[f"ogb{ob_}"])
            P.dma(og[i * 512 + qt * 128:i * 512 + (qt + 1) * 128, :], ogb[ob_][:], reads=[f"ogb{ob_}"])
    P.finish()
    return nc


def kernel(x, positions, hgrn_lb_logits,
           l0_w_in, l0_g_norm, l0_w_out, l0_ln_g, l0_ln_b,
           l1_w_in, l1_cmp_pos_k, l1_cmp_w1_k, l1_cmp_w2_k,
           l1_cmp_pos_v, l1_cmp_w1_v, l1_cmp_w2_v, l1_w_out, l1_ln_g, l1_ln_b,
           l2_w_in, l2_sinks, l2_w_out, l2_ln_g, l2_ln_b,
           l3_w_in, l3_g_norm, l3_w_out, l3_ln_g, l3_ln_b):
    f = lambda a: np.ascontiguousarray(np.asarray(a, dtype=np.float32))
    positions = np.ascontiguousarray(np.asarray(positions, dtype=np.int32))
    lbl = f(hgrn_lb_logits)
    h = f(x)[0]
    ogT = run_hgrn(np.ascontiguousarray(h.T), f(l0_w_in), f(l0_g_norm), lbl, False)
    h = run_out(ogT, h, f(l0_w_out), f(l0_ln_g), f(l0_ln_b))
    oa, ob = run_nsa_proj(np.ascontiguousarray(h.T), f(l1_w_in), positions)
    kc = oa[:, 2048:2560].reshape(S, 4, 128)
    vc = oa[:, 2560:3072].reshape(S, 4, 128)
    kcb, vcb = run_nsa_cmp(kc, vc, f(l1_cmp_pos_k), f(l1_cmp_w1_k), f(l1_cmp_w2_k),
                           f(l1_cmp_pos_v), f(l1_cmp_w1_v), f(l1_cmp_w2_v))
    ogT = run_nsa_attn(oa, ob, kcb, vcb)
    h = run_out(ogT, h, f(l1_w_out), f(l1_ln_g), f(l1_ln_b))
    ogT = run_swa(np.ascontiguousarray(h.T), f(l2_w_in), f(l2_sinks), positions)
    h = run_out(ogT, h, f(l2_w_out), f(l2_ln_g), f(l2_ln_b))
    ogT = run_hgrn(np.ascontiguousarray(h.T), f(l3_w_in), f(l3_g_norm), lbl, True)
    h = run_out(ogT, h, f(l3_w_out), f(l3_ln_g), f(l3_ln_b))
    return h[None].astype(np.float32)
```

```python
import numpy as np
import ml_dtypes
import concourse.bass as bass
import concourse.mybir as mybir
from concourse.bass_utils import run_bass_kernel_spmd

F32 = mybir.dt.float32
BF16 = mybir.dt.bfloat16
I32 = mybir.dt.int32
AF = mybir.ActivationFunctionType
ALU = mybir.AluOpType
AX = mybir.AxisListType

NCORES = 8
D = 2048
S = 8192
DEPTH = 4
ALPHA = (2 * DEPTH) ** 0.25
LN_EPS = 1e-5
RMS_EPS = 1e-6
BIG = 32768.0
SAME_ENG_SYNC = True


class Prog:
    CE = ("pe", "act", "dve", "pool")

    def __init__(self, nc, n_dma_sems=24, same_eng_sync=None):
        self.nc = nc
        self.same = SAME_ENG_SYNC if same_eng_sync is None else same_eng_sync
        self.sem = {e: nc.alloc_semaphore(name=f"s_{e}") for e in self.CE}
        self.cnt = {e: 0 for e in self.CE}
        self.dsem = [nc.alloc_semaphore(name=f"d{i}") for i in range(n_dma_sems)]
        self.dcnt = [0] * n_dma_sems
        self.rr = 0
        self.seen = {e: {} for e in self.CE + ("sp",)}
        self.streams = {e: [] for e in self.CE + ("sp",)}
        self.buf = {}
        self.ctx = []

    def sbuf(self, name, shape, dt):
        g = self.nc.sbuf_tensor(name, list(shape), dt)
        t = g.__enter__()
        self.ctx.append(g)
        return t

    def psum(self, name, shape, dt=F32):
        g = self.nc.psum_tensor(name, list(shape), dt)
        t = g.__enter__()
        self.ctx.append(g)
        return t

    def _deps(self, eng, reads, writes):
        deps = {}
        def add(ev):
            if ev is None:
                return
            sk, v = ev
            if sk == eng and (eng == "pe" or not self.same):
                return
            if deps.get(sk, 0) < v:
                deps[sk] = v
        for k in reads:
            b = self.buf.get(k)
            if b:
                add(b[0])
        for k in writes:
            b = self.buf.get(k)
            if b:
                add(b[0])
                for r in b[1]:
                    add(r)
        waits = []
        for sk, v in deps.items():
            if self.seen[eng].get(sk, 0) < v:
                self.seen[eng][sk] = v
                waits.append((sk, v))
        return waits

    def _record(self, ev, reads, writes):
        for k in reads:
            b = self.buf.setdefault(k, [None, []])
            b[1].append(ev)
        for k in writes:
            self.buf[k] = [ev, []]

    def op(self, eng, fn, reads=(), writes=()):
        waits = self._deps(eng, reads, writes)
        self.cnt[eng] += 1
        ev = (eng, self.cnt[eng])
        self.streams[eng].append((waits, fn, eng))
        self._record(ev, reads, writes)
        return ev

    def dma(self, out, in_, reads=(), writes=(), q="sp", **kw):
        i = self.rr
        self.rr = (self.rr + 1) % len(self.dsem)
        waits = self._deps(q, reads, writes)
        if self.dcnt[i] > 0:
            sk, v = ("dma", i), 16 * self.dcnt[i]
            if self.seen[q].get(sk, 0) < v:
                self.seen[q][sk] = v
                waits.append((sk, v))
        self.dcnt[i] += 1
        ev = (("dma", i), 16 * self.dcnt[i])
        self.streams[q].append((waits, lambda e: e.dma_start(out=out, in_=in_, **kw), ("dma", i)))
        self._record(ev, reads, writes)
        return ev

    def _semh(self, sk):
        return self.dsem[sk[1]] if isinstance(sk, tuple) else self.sem[sk]

    def finish(self):
        nc = self.nc
        final_waits = [(("dma", i), 16 * c) for i, c in enumerate(self.dcnt) if c > 0]
        streams = self.streams
        semh = self._semh
        sems = self.sem
        dsem = self.dsem

        def replay(e, name):
            for waits, fn, inc in streams[name]:
                for sk, v in waits:
                    e.wait_ge(semh(sk), v)
                ins = fn(e)
                if isinstance(inc, tuple):
                    ins.then_inc(dsem[inc[1]], 16)
                else:
                    ins.then_inc(sems[inc], 1)

        with nc.Block() as block:
            @block.sync
            def _(e):
                replay(e, "sp")
                for sk, v in final_waits:
                    e.wait_ge(semh(sk), v)

            @block.tensor
            def _(e):
                replay(e, "pe")

            @block.scalar
            def _(e):
                replay(e, "act")

            @block.vector
            def _(e):
                replay(e, "dve")

            @block.gpsimd
            def _(e):
                replay(e, "pool")
        for g in reversed(self.ctx):
            g.__exit__(None, None, None)


def _chunk_major(hT, tch=256):
    d_, s_ = hT.shape
    return np.ascontiguousarray(hT.reshape(d_ // 128, 128, s_ // tch, tch).transpose(2, 1, 0, 3))


def _bf16(a):
    return np.ascontiguousarray(a).astype(ml_dtypes.bfloat16)


def build_out_prog(ntok=1024):
    nc = bass.Bass("TRN2", target_bir_lowering=False)
    ogT = nc.dram_tensor("ogT", [D, ntok], BF16, kind="ExternalInput").ap()
    h = nc.dram_tensor("h", [ntok, D], F32, kind="ExternalInput").ap()
    w = nc.dram_tensor("w", [D, D], F32, kind="ExternalInput").ap()
    g = nc.dram_tensor("g", [1, D], F32, kind="ExternalInput").ap()
    b = nc.dram_tensor("b", [1, D], F32, kind="ExternalInput").ap()
    out = nc.dram_tensor("out", [ntok, D], F32, kind="ExternalOutput").ap()
    P = Prog(nc)
    KT = D // 128
    NT = ntok // 128
    og_sb = P.sbuf("og_sb", [128, KT, ntok], BF16)
    w_bf = P.sbuf("w_bf", [128, KT, D], BF16)
    wst = [P.sbuf(f"wst{i}", [128, 2, D], F32) for i in range(2)]
    g_sb = P.sbuf("g_sb", [128, D], F32)
    b_sb = P.sbuf("b_sb", [128, D], F32)
    hb = [P.sbuf(f"hb{i}", [128, D], F32) for i in range(2)]
    rb = [P.sbuf(f"rb{i}", [128, D], F32) for i in range(2)]
    st = [P.sbuf(f"st{i}", [128, 4, 6], F32) for i in range(2)]
    mv = [P.sbuf(f"mv{i}", [128, 2], F32) for i in range(2)]
    rs = [P.sbuf(f"rs{i}", [128, 1], F32) for i in range(2)]
    ps = [P.psum(f"ps{i}", [128, 512]) for i in range(4)]
    eps_sb = P.sbuf("eps_sb", [128, 1], F32)
    P.op("dve", lambda e: e.memset(eps_sb[:], LN_EPS), writes=["eps"])

    P.dma(og_sb[:], ogT.rearrange("(k p) t -> p k t", p=128), writes=["og"])
    P.dma(g_sb[:], g.partition_broadcast(128), writes=["g"])
    P.dma(b_sb[:], b.partition_broadcast(128), writes=["b"])
    wv = w.rearrange("(k p) n -> p k n", p=128)
    for k2 in range(KT // 2):
        s = k2 % 2
        P.dma(wst[s][:], wv[:, 2 * k2:2 * k2 + 2, :], writes=[f"wst{s}"])
        eng = "pool" if k2 % 2 else "act"
        if eng == "act":
            P.op("act", lambda e, s=s, k2=k2: e.copy(out=w_bf[:, 2 * k2:2 * k2 + 2, :], in_=wst[s][:]),
                 reads=[f"wst{s}"], writes=[f"wbf{k2}"])
        else:
            P.op("pool", lambda e, s=s, k2=k2: e.tensor_copy(out=w_bf[:, 2 * k2:2 * k2 + 2, :], in_=wst[s][:]),
                 reads=[f"wst{s}"], writes=[f"wbf{k2}"])
    wkeys = [f"wbf{k2}" for k2 in range(KT // 2)]
    for t in range(NT):
        s = t % 2
        P.dma(hb[s][:], h[t * 128:(t + 1) * 128, :], writes=[f"hb{s}"])
        for n in range(4):
            for k in range(KT):
                P.op("pe", lambda e, n=n, k=k, t=t: e.matmul(
                    ps[n][:], lhsT=og_sb[:, k, t * 128:(t + 1) * 128], rhs=w_bf[:, k, n * 512:(n + 1) * 512],
                    start=(k == 0), stop=(k == KT - 1)),
                    reads=["og"] + wkeys, writes=[f"ps{n}"])
            P.op("dve", lambda e, n=n, s=s: e.scalar_tensor_tensor(
                out=rb[s][:, n * 512:(n + 1) * 512], in0=hb[s][:, n * 512:(n + 1) * 512], scalar=ALPHA,
                in1=ps[n][:], op0=ALU.mult, op1=ALU.add),
                reads=[f"hb{s}", f"ps{n}"], writes=[f"rb{s}_{n}"])
            P.op("dve", lambda e, n=n, s=s: e.bn_stats(out=st[s][:, n, :], in_=rb[s][:, n * 512:(n + 1) * 512]),
                 reads=[f"rb{s}_{n}"], writes=[f"st{s}_{n}"])
        rkeys = [f"rb{s}_{n}" for n in range(4)]
        P.op("dve", lambda e, s=s: e.bn_aggr(out=mv[s][:], in_=st[s][:]),
             reads=[f"st{s}_{n}" for n in range(4)], writes=[f"mv{s}"])
        P.op("act", lambda e, s=s: e.activation(out=rs[s][:], in_=mv[s][:, 1:2], func=AF.Sqrt, bias=eps_sb[:, 0:1], scale=1.0),
             reads=[f"mv{s}", "eps"], writes=[f"rs{s}"])
        P.op("dve", lambda e, s=s: e.reciprocal(out=rs[s][:], in_=rs[s][:]),
             reads=[f"rs{s}"], writes=[f"rs{s}"])
        P.op("dve", lambda e, s=s: e.tensor_scalar(out=rb[s][:], in0=rb[s][:], scalar1=mv[s][:, 0:1],
                                                  scalar2=rs[s][:, 0:1], op0=ALU.subtract, op1=ALU.mult),
             reads=rkeys + [f"mv{s}", f"rs{s}"], writes=rkeys)
        P.op("pool", lambda e, s=s: e.tensor_tensor(out=rb[s][:], in0=rb[s][:], in1=g_sb[:], op=ALU.mult),
             reads=rkeys + ["g"], writes=rkeys)
        P.op("pool", lambda e, s=s: e.tensor_tensor(out=rb[s][:], in0=rb[s][:], in1=b_sb[:], op=ALU.add),
             reads=rkeys + ["b"], writes=rkeys)
        P.dma(out[t * 128:(t + 1) * 128, :], rb[s][:], reads=rkeys)
    P.finish()
    return nc


_PROGS = {}


def _get(name, builder):
    if name not in _PROGS:
        _PROGS[name] = builder()
    return _PROGS[name]


def run_out(ogT_full, h_full, w, g, b):
    nc = _get("out", build_out_prog)
    maps = []
    for c in range(NCORES):
        maps.append({
            "ogT": np.ascontiguousarray(ogT_full[:, c * 1024:(c + 1) * 1024]),
            "h": np.ascontiguousarray(h_full[c * 1024:(c + 1) * 1024]),
            "w": w, "g": g.reshape(1, D), "b": b.reshape(1, D),
        })
    res = run_bass_kernel_spmd(nc, maps, core_ids=list(range(NCORES)))
    return np.concatenate([r["out"] for r in res.results], axis=0)


TWO_PI = 2.0 * np.pi
CW1 = 6.28125
CW2 = float(np.float32(TWO_PI - 6.28125))


def emit_rotary_tables(P, pos_dram, half, rot, cos_sb, sin_sb, tag):
    NT = cos_sb.shape[1]
    posi = P.sbuf(f"posi{tag}", [128, NT], I32)
    posf = P.sbuf(f"posf{tag}", [128, NT], F32)
    ang = P.sbuf(f"ang{tag}", [128, NT, half], F32)
    yy = P.sbuf(f"yy{tag}", [128, NT, half], F32)
    ni = P.sbuf(f"ni{tag}", [128, NT, half], I32)
    nf = P.sbuf(f"nf{tag}", [128, NT, half], F32)
    rr = P.sbuf(f"rr{tag}", [128, NT, half], F32)
    mk = P.sbuf(f"mk{tag}", [128, NT, half], F32)
    k = f"rot{tag}"
    P.dma(posi[:], pos_dram, writes=[k + "posi"])
    P.op("dve", lambda e: e.tensor_copy(out=posf[:], in_=posi[:]), reads=[k + "posi"], writes=[k + "posf"])
    inv = [float(np.float32(500000.0) ** np.float32(-(2.0 * i) / rot)) for i in range(half)]
    for i in range(half):
        P.op("dve", lambda e, i=i: e.tensor_scalar(out=ang[:, :, i], in0=posf[:], scalar1=inv[i], scalar2=None,
                                                   op0=ALU.mult), reads=[k + "posf"], writes=[k + "ang"])

    def reduce_to(dst, src_key, shift):
        P.op("dve", lambda e: e.tensor_scalar(out=yy[:], in0=ang[:], scalar1=shift, scalar2=1.0 / TWO_PI,
                                              op0=ALU.add, op1=ALU.mult), reads=[k + "ang"], writes=[k + "yy"])
        P.op("dve", lambda e: e.tensor_copy(out=ni[:], in_=yy[:]), reads=[k + "yy"], writes=[k + "ni"])
        P.op("dve", lambda e: e.tensor_copy(out=nf[:], in_=ni[:]), reads=[k + "ni"], writes=[k + "nf"])
        P.op("dve", lambda e: e.scalar_tensor_tensor(out=rr[:], in0=nf[:], scalar=-CW1, in1=ang[:],
                                                     op0=ALU.mult, op1=ALU.add),
             reads=[k + "nf", k + "ang"], writes=[k + "rr"])
        P.op("dve", lambda e: e.scalar_tensor_tensor(out=rr[:], in0=nf[:], scalar=-CW2, in1=rr[:],
                                                     op0=ALU.mult, op1=ALU.add),
             reads=[k + "nf", k + "rr"], writes=[k + "rr"])
        if shift != 0.0:
            P.op("dve", lambda e: e.tensor_scalar(out=rr[:], in0=rr[:], scalar1=shift, scalar2=None, op0=ALU.add),
                 reads=[k + "rr"], writes=[k + "rr"])
        P.op("dve", lambda e: e.tensor_scalar(out=mk[:], in0=rr[:], scalar1=float(np.pi), scalar2=-TWO_PI,
                                              op0=ALU.is_gt, op1=ALU.mult), reads=[k + "rr"], writes=[k + "mk"])
        P.op("dve", lambda e: e.tensor_tensor(out=rr[:], in0=rr[:], in1=mk[:], op=ALU.add),
             reads=[k + "rr", k + "mk"], writes=[k + "rr"])
        P.op("dve", lambda e: e.tensor_scalar(out=mk[:], in0=rr[:], scalar1=-float(np.pi), scalar2=TWO_PI,
                                              op0=ALU.is_lt, op1=ALU.mult), reads=[k + "rr"], writes=[k + "mk"])
        P.op("dve", lambda e: e.tensor_tensor(out=rr[:], in0=rr[:], in1=mk[:], op=ALU.add),
             reads=[k + "rr", k + "mk"], writes=[k + "rr"])
        P.op("dve", lambda e: e.tensor_scalar(out=rr[:], in0=rr[:], scalar1=3.1415925, scalar2=-3.1415925,
                                              op0=ALU.min, op1=ALU.max), reads=[k + "rr"], writes=[k + "rr"])
        P.op("act", lambda e: e.activation(out=dst[:], in_=rr[:], func=AF.Sin), reads=[k + "rr"], writes=[src_key])

    reduce_to(sin_sb, k + "sin", 0.0)
    reduce_to(cos_sb, k + "cos", float(np.pi / 2))
    return k + "cos", k + "sin"


def emit_rotary(P, eng, src, dst, cos_t, sin_t, nh, half, tmp, rkeys, wkeys, ckeys):
    C = cos_t.unsqueeze(1).broadcast_to([128, nh, half])
    Sn = sin_t.unsqueeze(1).broadcast_to([128, nh, half])
    x1 = src[:, :, 0:half]
    x2 = src[:, :, half:2 * half]
    tk = wkeys[0] + "_tmp"
    P.op(eng, lambda e: e.tensor_tensor(out=tmp[:, 0], in0=x1, in1=C, op=ALU.mult), reads=rkeys + ckeys, writes=[tk + "0"])
    P.op(eng, lambda e: e.tensor_tensor(out=tmp[:, 1], in0=x2, in1=Sn, op=ALU.mult), reads=rkeys + ckeys, writes=[tk + "1"])
    P.op(eng, lambda e: e.tensor_tensor(out=tmp[:, 2], in0=x2, in1=C, op=ALU.mult), reads=rkeys + ckeys, writes=[tk + "2"])
    P.op(eng, lambda e: e.tensor_tensor(out=tmp[:, 3], in0=x1, in1=Sn, op=ALU.mult), reads=rkeys + ckeys, writes=[tk + "3"])
    P.op(eng, lambda e: e.tensor_tensor(out=dst[:, :, 0:half], in0=tmp[:, 0], in1=tmp[:, 1], op=ALU.subtract),
         reads=[tk + "0", tk + "1"], writes=[wkeys[0] + "_a"])
    P.op(eng, lambda e: e.tensor_tensor(out=dst[:, :, half:2 * half], in0=tmp[:, 2], in1=tmp[:, 3], op=ALU.add),
         reads=[tk + "2", tk + "3"], writes=[wkeys[0] + "_b"])
    return [wkeys[0] + "_a", wkeys[0] + "_b"]


def swa_consts():
    kk = np.arange(128)[:, None]
    qq = np.arange(128)[None, :]
    cur = np.where(kk <= qq, 0.0, -1.0)
    prev = np.where(kk > qq, 0.0, -1.0)
    m = np.stack([np.tile(prev, (1, 4)), np.tile(cur, (1, 4))], 0)
    ident = np.eye(128)
    return {"masks": _bf16(m), "ident": _bf16(ident), "bigi": _bf16(ident * BIG)}


def build_swa_prog(S=S, stage=9):
    nc = bass.Bass("TRN2", target_bir_lowering=False)
    TCH = 256
    NCH = S // TCH
    NT = S // 128
    KT = D // 128
    NW = 640
    hT = nc.dram_tensor("hT", [NCH, 128, KT, TCH], F32, kind="ExternalInput").ap()
    w = nc.dram_tensor("w", [D, NW], F32, kind="ExternalInput").ap()
    pos = nc.dram_tensor("pos", [128, S // 128], I32, kind="ExternalInput").ap()
    sinks = nc.dram_tensor("sinks", [1, 4], F32, kind="ExternalInput").ap()
    masks = nc.dram_tensor("masks", [2, 128, 512], BF16, kind="ExternalInput").ap()
    ident = nc.dram_tensor("ident", [128, 128], BF16, kind="ExternalInput").ap()
    bigi = nc.dram_tensor("bigi", [128, 128], BF16, kind="ExternalInput").ap()
    og = nc.dram_tensor("og", [S, 256], BF16, kind="ExternalOutput").ap()
    P = Prog(nc)
    hst = [P.sbuf(f"hst{i}", [128, KT, TCH], F32) for i in range(2)]
    hbf = [P.sbuf(f"hbf{i}", [128, KT, TCH], BF16) for i in range(2)]
    wst = P.sbuf("wst", [128, KT, NW], F32)
    wbf = P.sbuf("wbf", [128, KT, NW], BF16)
    cos_sb = P.sbuf("cos_sb", [128, NT, 8], F32)
    sin_sb = P.sbuf("sin_sb", [128, NT, 8], F32)
    mk_sb = P.sbuf("mk_sb", [128, 2, 512], BF16)
    id_sb = P.sbuf("id_sb", [128, 128], BF16)
    bi_sb = P.sbuf("bi_sb", [128, 128], BF16)
    snk = P.sbuf("snk", [128, 4], F32)
    esnk = P.sbuf("esnk", [128, 4], F32)
    vaug = P.sbuf("vaug", [128, NT, 65], BF16)
    kTl = P.sbuf("kTl", [128, NT, 128], BF16)
    kTh = P.sbuf("kTh", [128, NT, 128], BF16)
    qT = [P.sbuf(f"qT{i}", [128, 2, 128], BF16) for i in range(2)]
    qkb = [P.sbuf(f"qkb{i}", [128, 6, 64], BF16) for i in range(2)]
    rtmp = [P.sbuf(f"rtmp{i}", [128, 4, 5, 8], F32) for i in range(2)]
    qkf = [P.sbuf(f"qkf{i}", [128, 320], F32) for i in range(2)]
    zs = [P.sbuf(f"zs{i}", [128, 256], F32) for i in range(2)]
    Eb = [P.sbuf(f"Eb{i}", [128, 512], BF16) for i in range(4)]
    den = [P.sbuf(f"den{i}", [128, 4], F32) for i in range(2)]
    ob = [P.sbuf(f"ob{i}", [128, 4, 64], F32) for i in range(2)]
    osb = [P.sbuf(f"osb{i}", [128, 4, 65], F32) for i in range(2)]
    ogb = [P.sbuf(f"ogb{i}", [128, 256], BF16) for i in range(2)]
    psA = P.psum("psA", [128, 384])
    psB = P.psum("psB", [128, 256])
    psT = P.psum("psT", [128, 3, 128], BF16)
    psS = [P.psum(f"psS{i}", [128, 512]) for i in range(2)]
    psO = P.psum("psO", [128, 4, 65])

    P.dma(mk_sb[:], masks.rearrange("m p n -> p m n"), writes=["mk"])
    P.dma(id_sb[:], ident, writes=["id"])
    P.dma(bi_sb[:], bigi, writes=["bi"])
    P.dma(snk[:], sinks.partition_broadcast(128), writes=["snk"])
    P.dma(wst[:], w.rearrange("(k p) n -> p k n", p=128), writes=["wst"])
    ck, sk = emit_rotary_tables(P, pos, 8, 16, cos_sb, sin_sb, "s")
    P.op("act", lambda e: e.activation(out=esnk[:], in_=snk[:], func=AF.Exp), reads=["snk"], writes=["esnk"])
    P.op("pool", lambda e: e.tensor_copy(out=wbf[:, 0:8, :], in_=wst[:, 0:8, :]), reads=["wst"], writes=["wbf0"])
    P.op("act", lambda e: e.copy(out=wbf[:, 8:16, :], in_=wst[:, 8:16, :]), reads=["wst"], writes=["wbf1"])
    P.op("pool", lambda e: e.memset(vaug[:, :, 64:65], 1.0), writes=["vones"])
    P.op("pool", lambda e: e.memset(kTl[:], 0.0), writes=["kTz"])
    P.op("pool", lambda e: e.memset(kTh[:], 0.0), writes=["kTz2"])

    def load_chunk(ci):
        s = ci % 2
        P.dma(hst[s][:, 0:8, :], hT[ci][:, 0:8, :], writes=[f"hst{s}"])
        P.dma(hst[s][:, 8:16, :], hT[ci][:, 8:16, :], writes=[f"hst{s}_2"])
        P.op("pool", lambda e: e.tensor_copy(out=hbf[s][:, 0:8, :], in_=hst[s][:, 0:8, :]),
             reads=[f"hst{s}"], writes=[f"hbf{s}a"])
        P.op("act", lambda e: e.copy(out=hbf[s][:, 8:16, :], in_=hst[s][:, 8:16, :]),
             reads=[f"hst{s}_2"], writes=[f"hbf{s}b"])

    load_chunk(0)
    for ci in range(NCH):
        s = ci % 2
        if ci + 1 < NCH:
            load_chunk(ci + 1)
        for j in range(TCH // 128):
            t = ci * (TCH // 128) + j
            u = t % 2
            if stage < 2:
                continue
            for k in range(KT):
                P.op("pe", lambda e, k=k, j=j, s=s: e.matmul(
                    psA[:], lhsT=hbf[s][:, k, j * 128:(j + 1) * 128], rhs=wbf[:, k, 0:384],
                    start=(k == 0), stop=(k == KT - 1)),
                    reads=[f"hbf{s}a", f"hbf{s}b", "wbf0", "wbf1"], writes=["psA"])
            for k in range(KT):
                P.op("pe", lambda e, k=k, j=j, s=s: e.matmul(
                    psB[:], lhsT=hbf[s][:, k, j * 128:(j + 1) * 128], rhs=wbf[:, k, 384:640],
                    start=(k == 0), stop=(k == KT - 1)),
                    reads=[f"hbf{s}a", f"hbf{s}b", "wbf0", "wbf1"], writes=["psB"])
            src = psA[:, 0:320].rearrange("p (h d) -> p h d", d=64)
            P.op("act", lambda e, t=t: e.copy(out=vaug[:, t, 0:64], in_=psA[:, 320:384]),
                 reads=["psA"], writes=[f"v{t}"])
            rk = []
            if stage >= 2.2:
                P.op("act", lambda e, u=u: e.copy(out=qkf[u][:], in_=psA[:, 0:320]), reads=["psA"], writes=[f"qkf{u}"])
                srcs = qkf[u][:].rearrange("p (h d) -> p h d", d=64)
                rk = emit_rotary(P, "dve", srcs, qkb[u][:, 0:5, :], cos_sb[:, t, :], sin_sb[:, t, :], 5, 8, rtmp[u],
                                 [f"qkf{u}"], [f"qkb{u}"], [ck, sk])
            if stage >= 2.4:
                P.op("pool", lambda e, u=u, srcs=srcs: e.tensor_copy(out=qkb[u][:, 0:5, 16:64], in_=srcs[:, :, 16:64]),
                     reads=[f"qkf{u}"], writes=[f"qkb{u}_c"])
            if stage >= 2.6:
                P.op("act", lambda e, u=u: e.activation(out=zs[u][:], in_=psB[:], func=AF.Exp, scale=-1.0),
                     reads=["psB"], writes=[f"zs{u}"])
                P.op("pool", lambda e, u=u: e.tensor_scalar(out=zs[u][:], in0=zs[u][:], scalar1=1.0, scalar2=None, op0=ALU.add),
                     reads=[f"zs{u}"], writes=[f"zs{u}"])
                P.op("dve", lambda e, u=u: e.reciprocal(out=zs[u][:], in_=zs[u][:]), reads=[f"zs{u}"], writes=[f"zs{u}"])
                P.op("dve", lambda e, u=u: e.tensor_tensor(out=zs[u][:], in0=psB[:], in1=zs[u][:], op=ALU.mult),
                     reads=["psB", f"zs{u}"], writes=[f"zs{u}"])
            if stage >= 2.8:
                P.op("pool", lambda e, u=u: e.tensor_copy(out=qkb[u][:, 5, :], in_=qkb[u][:, 4, :]),
                     reads=rk + [f"qkb{u}_c"], writes=[f"qkb{u}_k2"])
            qkeys = rk + [f"qkb{u}_c", f"qkb{u}_k2"]
            if stage < 3:
                continue
            for g in range(3):
                P.op("pe", lambda e, g=g, u=u: e.transpose(
                    out=psT[:, g, :], in_=qkb[u][:, 2 * g:2 * g + 2, :].rearrange("p h d -> p (h d)"), identity=id_sb[:]),
                    reads=qkeys + ["id"], writes=["psT"])
            P.op("dve", lambda e, u=u: e.tensor_copy(out=qT[u][:], in_=psT[:, 0:2, :]),
                 reads=["psT"], writes=[f"qT{u}"])
            P.op("dve", lambda e, t=t: e.tensor_copy(out=kTl[0:64, t, :], in_=psT[0:64, 2, :]),
                 reads=["psT", "kTz"], writes=[f"kT{t}"])
            P.op("dve", lambda e, t=t: e.tensor_copy(out=kTh[64:128, t, :], in_=psT[64:128, 2, :]),
                 reads=["psT", "kTz2"], writes=[f"kTh{t}"])
            if stage < 4:
                continue
            kts = ([t - 1] if t > 0 else []) + [t]
            for kt in kts:
                mi = 0 if kt == t - 1 else 1
                pS = psS[mi]
                P.op("pe", lambda e, mi=mi, pS=pS: e.matmul(
                    pS[:], lhsT=bi_sb[:], rhs=mk_sb[:, mi, :], start=True, stop=False),
                    reads=["bi", "mk"], writes=[f"psS{mi}"])
                P.op("pe", lambda e, kt=kt, pS=pS, u=u: e.matmul(
                    pS[:, 0:256], lhsT=kTl[:, kt, :], rhs=qT[u][:].rearrange("p a t -> p (a t)"),
                    start=False, stop=False), reads=[f"kT{kt}", f"qT{u}"], writes=[f"psS{mi}"])
                P.op("pe", lambda e, kt=kt, pS=pS, u=u: e.matmul(
                    pS[:, 256:512], lhsT=kTh[:, kt, :], rhs=qT[u][:].rearrange("p a t -> p (a t)"),
                    start=False, stop=True), reads=[f"kTh{kt}", f"qT{u}"], writes=[f"psS{mi}"])
                eb = Eb[(t % 2) * 2 + mi]
                P.op("act", lambda e, eb=eb, pS=pS: e.activation(out=eb[:], in_=pS[:], func=AF.Exp, scale=0.125),
                     reads=[f"psS{mi}"], writes=[f"Eb{(t % 2) * 2 + mi}"])
            if stage < 5:
                continue
            colmap = [0, 2, 1, 3]
            for idx, kt in enumerate(kts):
                mi = 0 if kt == t - 1 else 1
                eb = Eb[(t % 2) * 2 + mi]
                for hh in range(4):
                    cb = colmap[hh]
                    st_ = (idx == 0 and hh == 0)
                    sp_ = (idx == len(kts) - 1 and hh == 3)
                    P.op("pe", lambda e, eb=eb, hh=hh, cb=cb, kt=kt, st_=st_, sp_=sp_: e.matmul(
                        psO[:, hh, :], lhsT=eb[:, cb * 128:(cb + 1) * 128], rhs=vaug[:, kt, :],
                        start=st_, stop=sp_),
                        reads=[f"Eb{(t % 2) * 2 + mi}", f"v{kt}", "vones"], writes=["psO"])
            if stage < 6:
                continue
            P.op("act", lambda e, u=u: e.copy(out=osb[u][:].rearrange("p h d -> p (h d)"), in_=psO[:].rearrange("p h d -> p (h d)")),
                 reads=["psO"], writes=[f"osb{u}"])
            P.op("dve", lambda e, u=u: e.tensor_tensor(out=den[u][:], in0=osb[u][:, :, 64], in1=esnk[:], op=ALU.add),
                 reads=[f"osb{u}", "esnk"], writes=[f"den{u}"])
            P.op("dve", lambda e, u=u: e.reciprocal(out=den[u][:], in_=den[u][:]), reads=[f"den{u}"], writes=[f"den{u}"])
            P.op("dve", lambda e, u=u: e.tensor_tensor(
                out=ob[u][:], in0=osb[u][:, :, 0:64], in1=den[u][:].unsqueeze(2).broadcast_to([128, 4, 64]), op=ALU.mult),
                reads=[f"osb{u}", f"den{u}"], writes=[f"ob{u}"])
            P.op("pool", lambda e, u=u: e.tensor_tensor(
                out=ogb[u][:], in0=ob[u][:].rearrange("p h d -> p (h d)"), in1=zs[u][:], op=ALU.mult),
                reads=[f"ob{u}", f"zs{u}"], writes=[f"ogb{u}"])
            P.dma(og[t * 128:(t + 1) * 128, :], ogb[u][:], reads=[f"ogb{u}"])
    P.finish()
    return nc


def run_swa(hT, w_in, sinks, positions):
    nc = _get("swa", build_swa_prog)
    cst = swa_consts()
    hT = _chunk_major(hT)
    maps = []
    for c in range(NCORES):
        kv = c // 2
        cols = np.concatenate([
            np.arange(c * 256, (c + 1) * 256),
            2048 + np.arange(kv * 64, (kv + 1) * 64),
            2048 + 256 + np.arange(kv * 64, (kv + 1) * 64),
            2048 + 512 + np.arange(c * 256, (c + 1) * 256)])
        maps.append({"hT": hT, "w": np.ascontiguousarray(w_in[:, cols]), "pos": np.ascontiguousarray(positions.reshape(S // 128, 128).T),
                     "sinks": np.ascontiguousarray(sinks[c * 4:(c + 1) * 4]).reshape(1, 4), **cst})
    res = run_bass_kernel_spmd(nc, maps, core_ids=list(range(NCORES)))
    og = np.concatenate([r["og"] for r in res.results], axis=1)
    return np.ascontiguousarray(og.T)


def hgrn_consts():
    s_ = np.arange(128)[:, None]
    t_ = np.arange(128)[None, :]
    same = (s_ // 64) == (t_ // 64)
    m1 = (same & (s_ <= t_)).astype(np.float32)
    m2 = (same & (s_ > t_)).astype(np.float32)
    ind = np.stack([(np.arange(128) < 64), (np.arange(128) >= 64)], 1).astype(np.float32)
    return {"m1": m1, "m2": m2, "ind": ind, "ident": _bf16(np.eye(128))}


def build_hgrn_prog(S=S, use_lb=False):
    nc = bass.Bass("TRN2", target_bir_lowering=False)
    TCH = 256
    NCH = S // TCH
    KT = D // 128
    NW = 1024
    hT = nc.dram_tensor("hT", [NCH, 128, KT, TCH], F32, kind="ExternalInput").ap()
    w = nc.dram_tensor("w", [D, NW], F32, kind="ExternalInput").ap()
    gn = nc.dram_tensor("gn", [1, 256], F32, kind="ExternalInput").ap()
    lbl = nc.dram_tensor("lbl", [2, 256], F32, kind="ExternalInput").ap()
    m1d = nc.dram_tensor("m1", [128, 128], F32, kind="ExternalInput").ap()
    m2d = nc.dram_tensor("m2", [128, 128], F32, kind="ExternalInput").ap()
    indd = nc.dram_tensor("ind", [128, 2], F32, kind="ExternalInput").ap()
    identd = nc.dram_tensor("ident", [128, 128], BF16, kind="ExternalInput").ap()
    og = nc.dram_tensor("og", [S, 256], BF16, kind="ExternalOutput").ap()
    P = Prog(nc)
    hst = [P.sbuf(f"hst{i}", [128, KT, TCH], F32) for i in range(2)]
    hbf = [P.sbuf(f"hbf{i}", [128, KT, TCH], BF16) for i in range(2)]
    wst = P.sbuf("wst", [128, 8, NW], F32)
    wbf = P.sbuf("wbf", [128, KT, NW], BF16)
    m1 = P.sbuf("m1s", [128, 128], F32)
    m2 = P.sbuf("m2s", [128, 128], F32)
    ind = P.sbuf("inds", [128, 2], F32)
    id_sb = P.sbuf("id_sb", [128, 128], BF16)
    gnb = P.sbuf("gnb", [128, 256], F32)
    lb0 = P.sbuf("lb0", [128, 256], F32)
    lb1 = P.sbuf("lb1", [128, 256], F32)
    lbt = P.sbuf("lbt", [128, 256], F32)
    oml = P.sbuf("oml", [128, 256], F32)
    epsb = P.sbuf("epsb", [128, 1], F32)
    Sf = [P.sbuf(f"Sf{h}", [128, 128], F32) for h in range(2)]
    Sb = [[P.sbuf(f"Sb{h}_{j}", [128, 128], BF16) for j in range(2)] for h in range(2)]
    NB = 3
    sg = [P.sbuf(f"sg{i}", [128, 256], F32) for i in range(NB)]
    lf = [P.sbuf(f"lf{i}", [128, 256], F32) for i in range(NB)]
    kk = [P.sbuf(f"kk{i}", [128, 256], F32) for i in range(NB)]
    eb = [P.sbuf(f"eb{i}", [128, 256], F32) for i in range(NB)]
    enb = [P.sbuf(f"enb{i}", [128, 256], F32) for i in range(NB)]
    ec = [P.sbuf(f"ec{i}", [128, 256], F32) for i in range(NB)]
    qt = [P.sbuf(f"qt{i}", [128, 256], BF16) for i in range(NB)]
    kt_ = [P.sbuf(f"kt{i}", [128, 256], BF16) for i in range(NB)]
    khA = [P.sbuf(f"khA{i}", [128, 256], BF16) for i in range(NB)]
    khB = [P.sbuf(f"khB{i}", [128, 256], BF16) for i in range(NB)]
    vb = [P.sbuf(f"vb{i}", [128, 256], BF16) for i in range(NB)]
    qf = [P.sbuf(f"qf{i}", [128, 256], F32) for i in range(NB)]
    zf = [P.sbuf(f"zf{i}", [128, 256], F32) for i in range(NB)]
    gz = [P.sbuf(f"gz{i}", [128, 256], F32) for i in range(NB)]
    dAB = [P.sbuf(f"dAB{i}", [128, 4], F32) for i in range(NB)]
    qT = [P.sbuf(f"qTf{i}", [128, 2, 128], BF16) for i in range(NB)]
    kT = [P.sbuf(f"kTf{i}", [128, 2, 128], BF16) for i in range(NB)]
    qTA = [P.sbuf(f"qTA{i}", [128, 2, 128], BF16) for i in range(NB)]
    qTB = [P.sbuf(f"qTB{i}", [128, 2, 128], BF16) for i in range(NB)]
    At = [P.sbuf(f"At{i}", [128, 2, 128], BF16) for i in range(NB)]
    osb = [P.sbuf(f"osb{i}", [128, 256], F32) for i in range(NB)]
    sq = [P.sbuf(f"sq{i}", [128, 128], F32) for i in range(NB)]
    ssq = [P.sbuf(f"ssq{i}", [128, 2], F32) for i in range(NB)]
    ogb = [P.sbuf(f"ogb{i}", [128, 256], BF16) for i in range(NB)]
    psQF = P.psum("psQF", [128, 512])
    psVZ = P.psum("psVZ", [128, 512])
    psBC = P.psum("psBC", [128, 512])
    psD = P.psum("psD", [128, 4])
    psT = P.psum("psT", [128, 4, 128], BF16)
    psA = P.psum("psA", [128, 2, 128])
    psO = P.psum("psO", [128, 2, 128])
    psU = P.psum("psU", [128, 2, 128])

    P.dma(m1[:], m1d, writes=["m1"])
    P.dma(m2[:], m2d, writes=["m2"])
    P.dma(ind[:], indd, writes=["ind"])
    P.dma(id_sb[:], identd, writes=["id"])
    P.dma(gnb[:], gn.partition_broadcast(128), writes=["gnb"])
    P.op("dve", lambda e: e.memset(epsb[:], RMS_EPS), writes=["epsb"])
    if use_lb:
        P.dma(lb0[:], lbl[0:1, :].partition_broadcast(128), writes=["lb0"])
        P.dma(lb1[:], lbl[1:2, :].partition_broadcast(128), writes=["lb1"])
        P.op("dve", lambda e: e.tensor_tensor(out=lb0[:], in0=lb1[:], in1=lb0[:], op=ALU.subtract),
             reads=["lb0", "lb1"], writes=["lb0"])
        P.op("act", lambda e: e.activation(out=lbt[:], in_=lb0[:], func=AF.Sigmoid), reads=["lb0"], writes=["lbt"])
        P.op("dve", lambda e: e.tensor_scalar(out=oml[:], in0=lbt[:], scalar1=-1.0, scalar2=1.0, op0=ALU.mult, op1=ALU.add),
             reads=["lbt"], writes=["oml"])
    wv = w.rearrange("(k p) n -> p k n", p=128)
    for half in range(2):
        P.dma(wst[:], wv[:, half * 8:(half + 1) * 8, :], writes=["wst"])
        P.op("pool", lambda e, half=half: e.tensor_copy(out=wbf[:, half * 8:half * 8 + 4, :], in_=wst[:, 0:4, :]),
             reads=["wst"], writes=[f"wbf{half}a"])
        P.op("act", lambda e, half=half: e.copy(out=wbf[:, half * 8 + 4:half * 8 + 8, :], in_=wst[:, 4:8, :]),
             reads=["wst"], writes=[f"wbf{half}b"])
    wkeys = ["wbf0a", "wbf0b", "wbf1a", "wbf1b"]
    for h in range(2):
        P.op("pool", lambda e, h=h: e.memset(Sf[h][:], 0.0), writes=[f"Sf{h}"])
        P.op("pool", lambda e, h=h: e.memset(Sb[h][0][:], 0.0), writes=[f"Sb{h}_0"])
    for i in range(NB):
        P.op("pool", lambda e, i=i: e.memset(qTA[i][:], 0.0), writes=[f"qTA{i}"])
        P.op("pool", lambda e, i=i: e.memset(qTB[i][:], 0.0), writes=[f"qTB{i}"])

    def load_chunk(ci):
        s = ci % 2
        P.dma(hst[s][:, 0:8, :], hT[ci][:, 0:8, :], writes=[f"hst{s}"])
        P.dma(hst[s][:, 8:16, :], hT[ci][:, 8:16, :], writes=[f"hst{s}_2"])
        P.op("pool", lambda e: e.tensor_copy(out=hbf[s][:, 0:8, :], in_=hst[s][:, 0:8, :]),
             reads=[f"hst{s}"], writes=[f"hbf{s}a"])
        P.op("act", lambda e: e.copy(out=hbf[s][:, 8:16, :], in_=hst[s][:, 8:16, :]),
             reads=[f"hst{s}_2"], writes=[f"hbf{s}b"])

    tiles = [(ci, j) for ci in range(NCH) for j in range(TCH // 128)]
    NTL = len(tiles)
    loaded = [0]

    def ensure_loaded(ci):
        while loaded[0] <= min(ci, NCH - 1):
            load_chunk(loaded[0])
            loaded[0] += 1

    def proj_ops(t):
        ci, j = tiles[t]
        s_ = ci % 2
        hk = [f"hbf{s_}a", f"hbf{s_}b"] + wkeys
        ops = []
        for k in range(KT):
            ops.append(lambda k=k, j=j, s_=s_, hk=hk: P.op("pe", lambda e: e.matmul(
                psQF[:], lhsT=hbf[s_][:, k, j * 128:(j + 1) * 128], rhs=wbf[:, k, 0:512],
                start=(k == 0), stop=(k == KT - 1)), reads=hk, writes=["psQF"]))
        for k in range(KT):
            ops.append(lambda k=k, j=j, s_=s_, hk=hk: P.op("pe", lambda e: e.matmul(
                psVZ[:], lhsT=hbf[s_][:, k, j * 128:(j + 1) * 128], rhs=wbf[:, k, 512:1024],
                start=(k == 0), stop=(k == KT - 1)), reads=hk, writes=["psVZ"]))
        return ops

    def stage_a(t):
        u = t % NB
        U = str(u)
        P.op("act", lambda e: e.copy(out=qf[u][:], in_=psQF[:, 0:256]), reads=["psQF"], writes=["qf" + U])
        P.op("act", lambda e: e.activation(out=sg[u][:], in_=psQF[:, 256:512], func=AF.Exp, scale=-1.0),
             reads=["psQF"], writes=["sg" + U])
        P.op("act", lambda e: e.copy(out=vb[u][:], in_=psVZ[:, 0:256]), reads=["psVZ"], writes=["vb" + U])
        P.op("act", lambda e: e.copy(out=zf[u][:], in_=psVZ[:, 256:512]), reads=["psVZ"], writes=["zf" + U])
        P.op("act", lambda e: e.activation(out=gz[u][:], in_=psVZ[:, 256:512], func=AF.Exp, scale=-1.0),
             reads=["psVZ"], writes=["gz" + U])
        P.op("pool", lambda e: e.tensor_scalar(out=sg[u][:], in0=sg[u][:], scalar1=1.0, scalar2=None, op0=ALU.add),
             reads=["sg" + U], writes=["sg" + U])
        P.op("dve", lambda e: e.reciprocal(out=sg[u][:], in_=sg[u][:]), reads=["sg" + U], writes=["sg" + U])
        if use_lb:
            P.op("dve", lambda e: e.tensor_tensor(out=sg[u][:], in0=sg[u][:], in1=oml[:], op=ALU.mult),
                 reads=["sg" + U, "oml"], writes=["sg" + U])
            P.op("dve", lambda e: e.tensor_tensor(out=sg[u][:], in0=sg[u][:], in1=lbt[:], op=ALU.add),
                 reads=["sg" + U, "lbt"], writes=["sg" + U])
        P.op("act", lambda e: e.activation(out=lf[u][:], in_=sg[u][:], func=AF.Ln), reads=["sg" + U], writes=["lf" + U])
        P.op("dve", lambda e: e.tensor_scalar(out=kk[u][:], in0=sg[u][:], scalar1=-1.0, scalar2=1.0,
                                              op0=ALU.mult, op1=ALU.add), reads=["sg" + U], writes=["kk" + U])
        P.op("pool", lambda e: e.tensor_scalar(out=gz[u][:], in0=gz[u][:], scalar1=1.0, scalar2=None, op0=ALU.add),
             reads=["gz" + U], writes=["gz" + U])
        P.op("dve", lambda e: e.reciprocal(out=gz[u][:], in_=gz[u][:]), reads=["gz" + U], writes=["gz" + U])
        P.op("pool", lambda e: e.tensor_tensor(out=gz[u][:], in0=zf[u][:], in1=gz[u][:], op=ALU.mult),
             reads=["zf" + U, "gz" + U], writes=["gz" + U])
        P.op("pool", lambda e: e.tensor_tensor(out=gz[u][:], in0=gz[u][:], in1=gnb[:], op=ALU.mult),
             reads=["gz" + U, "gnb"], writes=["gz" + U])

    def stage_b(t, fill):
        u = t % NB
        U = str(u)

        def pump(n):
            for _ in range(n):
                if fill:
                    fill.pop(0)()
        pump(8)
        P.op("pe", lambda e: e.matmul(psBC[:, 0:256], lhsT=m1[:], rhs=lf[u][:], start=True, stop=False),
             reads=["m1", "lf" + U], writes=["psBC"])
        P.op("pe", lambda e: e.matmul(psBC[:, 256:512], lhsT=m2[:], rhs=lf[u][:], start=False, stop=True),
             reads=["m2", "lf" + U], writes=["psBC"])
        P.op("pe", lambda e: e.matmul(psD[:, 0:2], lhsT=lf[u][:, 0:128], rhs=ind[:], start=True, stop=False),
             reads=["ind", "lf" + U], writes=["psD"])
        P.op("pe", lambda e: e.matmul(psD[:, 2:4], lhsT=lf[u][:, 128:256], rhs=ind[:], start=False, stop=True),
             reads=["ind", "lf" + U], writes=["psD"])
        pump(8)
        P.op("act", lambda e: e.activation(out=eb[u][:], in_=psBC[:, 0:256], func=AF.Exp), reads=["psBC"], writes=["eb" + U])
        P.op("act", lambda e: e.activation(out=enb[u][:], in_=psBC[:, 0:256], func=AF.Exp, scale=-1.0),
             reads=["psBC"], writes=["enb" + U])
        P.op("act", lambda e: e.activation(out=ec[u][:], in_=psBC[:, 256:512], func=AF.Exp), reads=["psBC"], writes=["ec" + U])
        P.op("act", lambda e: e.activation(out=dAB[u][:], in_=psD[:], func=AF.Exp), reads=["psD"], writes=["dAB" + U])
        P.op("dve", lambda e: e.tensor_tensor(out=qt[u][:], in0=qf[u][:], in1=eb[u][:], op=ALU.mult),
             reads=["qf" + U, "eb" + U], writes=["qt" + U])
        P.op("pool", lambda e: e.tensor_tensor(out=kt_[u][:], in0=kk[u][:], in1=enb[u][:], op=ALU.mult),
             reads=["kk" + U, "enb" + U], writes=["kt" + U])
        P.op("dve", lambda e: e.scalar_tensor_tensor(out=khA[u][:], in0=kk[u][:], scalar=ind[:, 0:1], in1=ec[u][:],
                                                     op0=ALU.mult, op1=ALU.mult),
             reads=["kk" + U, "ec" + U, "ind"], writes=["khA" + U])
        P.op("dve", lambda e: e.scalar_tensor_tensor(out=khB[u][:], in0=kk[u][:], scalar=ind[:, 1:2], in1=ec[u][:],
                                                     op0=ALU.mult, op1=ALU.mult),
             reads=["kk" + U, "ec" + U, "ind"], writes=["khB" + U])
        for h in range(2):
            P.op("pe", lambda e, h=h: e.transpose(out=psT[:, h, :], in_=qt[u][:, h * 128:(h + 1) * 128], identity=id_sb[:]),
                 reads=["qt" + U, "id"], writes=["psT"])
            P.op("pe", lambda e, h=h: e.transpose(out=psT[:, 2 + h, :], in_=kt_[u][:, h * 128:(h + 1) * 128], identity=id_sb[:]),
                 reads=["kt" + U, "id"], writes=["psT"])
        pump(8)
        P.op("dve", lambda e: e.tensor_copy(out=qT[u][:].rearrange("p a t -> p (a t)"),
                                            in_=psT[:, 0:2, :].rearrange("p a t -> p (a t)")), reads=["psT"], writes=["qT" + U])
        P.op("dve", lambda e: e.tensor_copy(out=kT[u][:].rearrange("p a t -> p (a t)"),
                                            in_=psT[:, 2:4, :].rearrange("p a t -> p (a t)")), reads=["psT"], writes=["kT" + U])
        P.op("pool", lambda e: e.tensor_copy(out=qTA[u][:, :, 0:64], in_=qT[u][:, :, 0:64]), reads=["qT" + U], writes=["qTA" + U])
        P.op("pool", lambda e: e.tensor_copy(out=qTB[u][:, :, 64:128], in_=qT[u][:, :, 64:128]), reads=["qT" + U], writes=["qTB" + U])
        for h in range(2):
            P.op("pe", lambda e, h=h: e.matmul(psA[:, h, :], lhsT=kT[u][:, h, :], rhs=qT[u][:, h, :], start=True, stop=True),
                 reads=["kT" + U, "qT" + U], writes=["psA"])
            P.op("dve", lambda e, h=h: e.tensor_tensor(out=At[u][:, h, :], in0=psA[:, h, :], in1=m1[:], op=ALU.mult),
                 reads=["psA", "m1"], writes=[f"At{u}_{h}"])
        for h in range(2):
            hs = slice(h * 128, (h + 1) * 128)
            P.op("pe", lambda e, h=h, hs=hs: e.matmul(psU[:, h, :], lhsT=khA[u][:, hs], rhs=vb[u][:, hs], start=True, stop=True),
                 reads=["khA" + U, "vb" + U], writes=["psU"])
            pump(4)
            P.op("dve", lambda e, h=h: e.scalar_tensor_tensor(
                out=Sf[h][:], in0=Sf[h][:], scalar=dAB[u][:, 2 * h:2 * h + 1], in1=psU[:, h, :],
                op0=ALU.mult, op1=ALU.add), reads=[f"Sf{h}", "dAB" + U, "psU"], writes=[f"Sf{h}"])
            P.op("act", lambda e, h=h: e.copy(out=Sb[h][1][:], in_=Sf[h][:]), reads=[f"Sf{h}"], writes=[f"Sb{h}_1"])
            P.op("pe", lambda e, h=h, hs=hs: e.matmul(psO[:, h, :], lhsT=At[u][:, h, :], rhs=vb[u][:, hs], start=True, stop=False),
                 reads=[f"At{u}_{h}", "vb" + U], writes=["psO"])
            P.op("pe", lambda e, h=h: e.matmul(psO[:, h, :], lhsT=qTA[u][:, h, :], rhs=Sb[h][0][:], start=False, stop=False),
                 reads=["qTA" + U, f"Sb{h}_0"], writes=["psO"])
            P.op("pe", lambda e, h=h: e.matmul(psO[:, h, :], lhsT=qTB[u][:, h, :], rhs=Sb[h][1][:], start=False, stop=True),
                 reads=["qTB" + U, f"Sb{h}_1"], writes=["psO"])
            P.op("pe", lambda e, h=h, hs=hs: e.matmul(psU[:, h, :], lhsT=khB[u][:, hs], rhs=vb[u][:, hs], start=True, stop=True),
                 reads=["khB" + U, "vb" + U], writes=["psU"])
            P.op("dve", lambda e, h=h: e.scalar_tensor_tensor(
                out=Sf[h][:], in0=Sf[h][:], scalar=dAB[u][:, 2 * h + 1:2 * h + 2], in1=psU[:, h, :],
                op0=ALU.mult, op1=ALU.add), reads=[f"Sf{h}", "dAB" + U, "psU"], writes=[f"Sf{h}"])
            P.op("act", lambda e, h=h: e.copy(out=Sb[h][0][:], in_=Sf[h][:]), reads=[f"Sf{h}"], writes=[f"Sb{h}_0"])
            P.op("act", lambda e, h=h, hs=hs: e.copy(out=osb[u][:, hs], in_=psO[:, h, :]), reads=["psO"], writes=[f"osb{u}_{h}"])
            P.op("dve", lambda e, h=h, hs=hs: e.tensor_tensor(out=sq[u][:], in0=osb[u][:, hs], in1=osb[u][:, hs], op=ALU.mult),
                 reads=[f"osb{u}_{h}"], writes=["sq" + U])
            P.op("dve", lambda e, h=h: e.reduce_sum(out=ssq[u][:, h:h + 1], in_=sq[u][:], axis=AX.X),
                 reads=["sq" + U], writes=[f"ssq{u}_{h}"])
            P.op("act", lambda e, h=h: e.activation(out=ssq[u][:, h:h + 1], in_=ssq[u][:, h:h + 1], func=AF.Ln,
                                                    bias=epsb[:, 0:1], scale=1.0 / 128.0),
                 reads=[f"ssq{u}_{h}", "epsb"], writes=[f"ssq{u}_{h}"])
            P.op("act", lambda e, h=h: e.activation(out=ssq[u][:, h:h + 1], in_=ssq[u][:, h:h + 1], func=AF.Exp, scale=-0.5),
                 reads=[f"ssq{u}_{h}"], writes=[f"ssq{u}_{h}"])
            P.op("dve", lambda e, h=h, hs=hs: e.scalar_tensor_tensor(
                out=ogb[u][:, hs], in0=osb[u][:, hs], scalar=ssq[u][:, h:h + 1], in1=gz[u][:, hs],
                op0=ALU.mult, op1=ALU.mult), reads=[f"osb{u}_{h}", f"ssq{u}_{h}", "gz" + U], writes=[f"ogb{u}_{h}"])
        pump(64)
        P.dma(og[t * 128:(t + 1) * 128, :], ogb[u][:], reads=[f"ogb{u}_0", f"ogb{u}_1"])

    ensure_loaded(1)
    for f_ in proj_ops(0):
        f_()
    stage_a(0)
    for t in range(NTL):
        fill = []
        if t + 1 < NTL:
            ensure_loaded(tiles[t + 1][0] + 1)
            fill = proj_ops(t + 1)
        stage_b(t, fill)
        if t + 1 < NTL:
            stage_a(t + 1)
    P.finish()
    return nc


def run_hgrn(hT, w_in, g_norm, lb_logits, use_lb):
    nc = _get("hgrn%d" % int(use_lb), lambda: build_hgrn_prog(use_lb=use_lb))
    cst = hgrn_consts()
    hT = _chunk_major(hT)
    maps = []
    for c in range(NCORES):
        cs = np.arange(c * 256, (c + 1) * 256)
        cols = np.concatenate([cs, 2048 + cs, 4096 + cs, 6144 + cs])
        maps.append({"hT": hT, "w": np.ascontiguousarray(w_in[:, cols]),
                     "gn": np.ascontiguousarray(g_norm[cs]).reshape(1, 256),
                     "lbl": np.ascontiguousarray(lb_logits[:, cs]), **cst})
    res = run_bass_kernel_spmd(nc, maps, core_ids=list(range(NCORES)))
    og = np.concatenate([r["og"] for r in res.results], axis=1)
    return np.ascontiguousarray(og.T)


NSA_IN = 7216


def build_nsa_proj_prog(ntok=1024):
    nc = bass.Bass("TRN2", target_bir_lowering=False)
    KT = D // 128
    NT = ntok // 128
    hT = nc.dram_tensor("hT", [D, ntok], F32, kind="ExternalInput").ap()
    w = nc.dram_tensor("w", [15, 128, KT, 512], F32, kind="ExternalInput").ap()
    pos = nc.dram_tensor("pos", [128, NT], I32, kind="ExternalInput").ap()
    oa = nc.dram_tensor("oa", [ntok, 5120], BF16, kind="ExternalOutput").ap()
    ob = nc.dram_tensor("ob", [ntok, 2096], F32, kind="ExternalOutput").ap()
    P = Prog(nc)
    hst = P.sbuf("hst", [128, 8, ntok], F32)
    hbf = P.sbuf("hbf", [128, KT, ntok], BF16)
    wst = P.sbuf("wst", [128, KT, 512], F32)
    wbf = [P.sbuf(f"wbf{i}", [128, KT, 512], BF16) for i in range(2)]
    cos_sb = P.sbuf("cos_sb", [128, NT, 16], F32)
    sin_sb = P.sbuf("sin_sb", [128, NT, 16], F32)
    ev = [P.sbuf(f"ev{i}", [128, 512], F32) for i in range(2)]
    o16 = [P.sbuf(f"o16{i}", [128, 512], BF16) for i in range(2)]
    rtmp = [P.sbuf(f"rtmp{i}", [128, 4, 4, 16], F32) for i in range(2)]
    ps = [P.psum(f"ps{i}", [128, 512]) for i in range(2)]
    hTv = hT.rearrange("(k p) t -> p k t", p=128)
    for half in range(2):
        P.dma(hst[:], hTv[:, half * 8:(half + 1) * 8, :], writes=["hst"])
        P.op("pool", lambda e, half=half: e.tensor_copy(out=hbf[:, half * 8:half * 8 + 4, :], in_=hst[:, 0:4, :]),
             reads=["hst"], writes=[f"hbf{half}a"])
        P.op("act", lambda e, half=half: e.copy(out=hbf[:, half * 8 + 4:half * 8 + 8, :], in_=hst[:, 4:8, :]),
             reads=["hst"], writes=[f"hbf{half}b"])
    hkeys = ["hbf0a", "hbf0b", "hbf1a", "hbf1b"]
    ck, sk = emit_rotary_tables(P, pos, 16, 32, cos_sb, sin_sb, "n")
    chunks = [(i * 512, 512, "rot") for i in range(4)]
    chunks += [(2048, 512, "rot"), (2560, 512, "plain"), (3072, 512, "rot"), (3584, 512, "plain"),
               (4096, 512, "rot"), (4608, 512, "plain"), (5120, 48, "f32")]
    chunks += [(5168 + i * 512, 512, "f32") for i in range(4)]
    it = 0
    for ci, (c0, ncol, kind) in enumerate(chunks):
        wb = ci % 2
        for q4 in range(4):
            P.dma(wst[:, 4 * q4:4 * q4 + 4, 0:ncol], w[ci][:, 4 * q4:4 * q4 + 4, 0:ncol], writes=[f"wst_{q4}"])
        P.op("pool", lambda e, wb=wb, ncol=ncol: e.tensor_copy(out=wbf[wb][:, 0:8, 0:ncol], in_=wst[:, 0:8, 0:ncol]),
             reads=["wst_0", "wst_1"], writes=[f"wbf{wb}a"])
        P.op("act", lambda e, wb=wb, ncol=ncol: e.copy(out=wbf[wb][:, 8:16, 0:ncol], in_=wst[:, 8:16, 0:ncol]),
             reads=["wst_2", "wst_3"], writes=[f"wbf{wb}b"])
        for t in range(NT):
            u = it % 2
            it += 1
            U = str(u)
            for k in range(KT):
                P.op("pe", lambda e, k=k, t=t, u=u, wb=wb, ncol=ncol: e.matmul(
                    ps[u][:, 0:ncol], lhsT=hbf[:, k, t * 128:(t + 1) * 128], rhs=wbf[wb][:, k, 0:ncol],
                    start=(k == 0), stop=(k == KT - 1)),
                    reads=hkeys + [f"wbf{wb}a", f"wbf{wb}b"], writes=["ps" + U])
            rows = slice(t * 128, (t + 1) * 128)
            if kind == "f32":
                P.op("act", lambda e, u=u, ncol=ncol: e.copy(out=ev[u][:, 0:ncol], in_=ps[u][:, 0:ncol]),
                     reads=["ps" + U], writes=["ev" + U])
                P.dma(ob[rows, c0 - 5120:c0 - 5120 + ncol], ev[u][:, 0:ncol], reads=["ev" + U])
            elif kind == "plain":
                P.op("act", lambda e, u=u: e.copy(out=o16[u][:], in_=ps[u][:]), reads=["ps" + U], writes=["o16" + U])
                P.dma(oa[rows, c0:c0 + 512], o16[u][:], reads=["o16" + U])
            else:
                P.op("act", lambda e, u=u: e.copy(out=ev[u][:], in_=ps[u][:]), reads=["ps" + U], writes=["ev" + U])
                src = ev[u][:].rearrange("p (h d) -> p h d", d=128)
                dst = o16[u][:].rearrange("p (h d) -> p h d", d=128)
                rk = emit_rotary(P, "dve", src, dst, cos_sb[:, t, :], sin_sb[:, t, :], 4, 16, rtmp[u],
                                 ["ev" + U], ["o16" + U], [ck, sk])
                P.op("pool", lambda e, src=src, dst=dst: e.tensor_copy(out=dst[:, :, 32:128], in_=src[:, :, 32:128]),
                     reads=["ev" + U], writes=["o16" + U + "_c"])
                P.dma(oa[rows, c0:c0 + 512], o16[u][:], reads=rk + ["o16" + U + "_c"])
    P.finish()
    return nc


def run_nsa_proj(hT, w_in, positions):
    nc = _get("nsa_proj", build_nsa_proj_prog)
    starts = [i * 512 for i in range(10)] + [5120] + [5168 + i * 512 for i in range(4)]
    wc = np.zeros((15, 128, D // 128, 512), np.float32)
    for ci, c0 in enumerate(starts):
        ncol = 48 if c0 == 5120 else 512
        wc[ci, :, :, :ncol] = w_in[:, c0:c0 + ncol].reshape(D // 128, 128, ncol).transpose(1, 0, 2)
    w_in = wc
    maps = []
    for c in range(NCORES):
        ts = slice(c * 1024, (c + 1) * 1024)
        maps.append({"hT": np.ascontiguousarray(hT[:, ts]), "w": w_in,
                     "pos": np.ascontiguousarray(positions.reshape(-1)[ts].reshape(8, 128).T)})
    res = run_bass_kernel_spmd(nc, maps, core_ids=list(range(NCORES)))
    oa = np.concatenate([r["oa"] for r in res.results], axis=0)
    ob = np.concatenate([r["ob"] for r in res.results], axis=0)
    return oa, ob


def build_nsa_cmp_prog():
    nc = bass.Bass("TRN2", target_bir_lowering=False)
    x = nc.dram_tensor("x", [128, 16, 512], BF16, kind="ExternalInput").ap()
    w1 = nc.dram_tensor("w1", [4096, 256], F32, kind="ExternalInput").ap()
    w2 = nc.dram_tensor("w2", [256, 128], F32, kind="ExternalInput").ap()
    posT = nc.dram_tensor("posT", [128, 32], F32, kind="ExternalInput").ap()
    out = nc.dram_tensor("out", [512, 128], F32, kind="ExternalOutput").ap()
    P = Prog(nc)
    x_sb = P.sbuf("x_sb", [128, 16, 512], BF16)
    w1s = P.sbuf("w1s", [128, 32, 256], F32)
    w1b = P.sbuf("w1b", [128, 32, 256], BF16)
    w2s = P.sbuf("w2s", [128, 2, 128], F32)
    w2b = P.sbuf("w2b", [128, 2, 128], BF16)
    pts = P.sbuf("pts", [128, 32], F32)
    ptb = P.sbuf("ptb", [128, 32], BF16)
    bias = P.sbuf("bias", [128, 2], F32)
    hid = P.sbuf("hid", [128, 2, 512], BF16)
    osb = [P.sbuf(f"osb{i}", [128, 128], F32) for i in range(2)]
    psH = [P.psum(f"psH{i}", [128, 512]) for i in range(2)]
    psB = P.psum("psB", [128, 2])
    psO = [P.psum(f"psO{i}", [128, 128]) for i in range(2)]
    P.dma(x_sb[:], x, writes=["x"])
    P.dma(w1s[:], w1.rearrange("(j p) n -> p j n", p=128), writes=["w1s"])
    P.dma(w2s[:], w2.rearrange("(k p) n -> p k n", p=128), writes=["w2s"])
    P.dma(pts[:], posT, writes=["pts"])
    P.op("pool", lambda e: e.tensor_copy(out=w1b[:, 0:16, :], in_=w1s[:, 0:16, :]), reads=["w1s"], writes=["w1ba"])
    P.op("act", lambda e: e.copy(out=w1b[:, 16:32, :], in_=w1s[:, 16:32, :]), reads=["w1s"], writes=["w1bb"])
    P.op("dve", lambda e: e.tensor_copy(out=w2b[:], in_=w2s[:]), reads=["w2s"], writes=["w2b"])
    P.op("dve", lambda e: e.tensor_copy(out=ptb[:], in_=pts[:]), reads=["pts"], writes=["ptb"])
    P.op("pool", lambda e: e.memset(hid[:], 0.0), writes=["hid0", "hid1"])
    wk = ["w1ba", "w1bb"]
    for hf in range(2):
        for j in range(32):
            P.op("pe", lambda e, hf=hf, j=j: e.matmul(psB[:, hf:hf + 1], lhsT=w1b[:, j, hf * 128:(hf + 1) * 128],
                                                      rhs=ptb[:, j:j + 1], start=(j == 0 and hf == 0), stop=(j == 31 and hf == 1)),
                 reads=wk + ["ptb"], writes=["psB"])
    P.op("dve", lambda e: e.tensor_copy(out=bias[:], in_=psB[:]), reads=["psB"], writes=["bias"])
    for hf in range(2):
        for j in range(32):
            P.op("pe", lambda e, hf=hf, j=j: e.matmul(
                psH[hf][:, 0:511], lhsT=w1b[:, j, hf * 128:(hf + 1) * 128],
                rhs=x_sb[:, j % 16, (j // 16):(j // 16) + 511], start=(j == 0), stop=(j == 31)),
                reads=wk + ["x"], writes=[f"psH{hf}"])
        P.op("act", lambda e, hf=hf: e.activation(out=hid[:, hf, 0:511], in_=psH[hf][:, 0:511], func=AF.Silu,
                                                  bias=bias[:, hf:hf + 1], scale=1.0),
             reads=[f"psH{hf}", "bias", f"hid{hf}"], writes=[f"hid{hf}"])
    for it in range(4):
        u = it % 2
        for hf in range(2):
            P.op("pe", lambda e, it=it, hf=hf, u=u: e.matmul(psO[u][:], lhsT=hid[:, hf, it * 128:(it + 1) * 128],
                                                            rhs=w2b[:, hf, :], start=(hf == 0), stop=(hf == 1)),
                 reads=["hid0", "hid1", "w2b"], writes=[f"psO{u}"])
        P.op("act", lambda e, u=u: e.copy(out=osb[u][:], in_=psO[u][:]), reads=[f"psO{u}"], writes=[f"osb{u}"])
        P.dma(out[it * 128:(it + 1) * 128, :], osb[u][:], reads=[f"osb{u}"])
    P.finish()
    return nc


def run_nsa_cmp(kc, vc, posk, w1k, w2k, posv, w1v, w2v):
    nc = _get("nsa_cmp", build_nsa_cmp_prog)
    maps = []
    for c in range(NCORES):
        g, isv = c // 2, c % 2
        src = vc if isv else kc
        xg = src[:, g, :]
        xr = np.ascontiguousarray(xg.reshape(512, 16, 128).transpose(2, 1, 0))
        maps.append({"x": xr, "w1": w1v if isv else w1k, "w2": w2v if isv else w2k,
                     "posT": np.ascontiguousarray((posv if isv else posk).T)})
    res = run_bass_kernel_spmd(nc, maps, core_ids=list(range(NCORES)))
    kcb = np.stack([res.results[2 * g]["out"] for g in range(4)], 0)
    vcb = np.stack([res.results[2 * g + 1]["out"] for g in range(4)], 0)
    return kcb, vcb


NSA_SCALE = 128.0 ** -0.5
NEG = -1.0e30


def nsa_core_inputs(oa, ob, kcb, vcb, g, pair, chunks, cst, bonus):
    tok = np.concatenate([np.arange(c * 512, (c + 1) * 512) for c in chunks])
    horder = [2 * pair, 2 * pair + 1, 2 * (1 - pair), 2 * (1 - pair) + 1]
    q = oa[tok][:, g * 512:(g + 1) * 512].reshape(len(tok), 4, 128)[:, horder, :]
    gcols = np.array([b * 16 + 4 * g + hh for b in range(3) for hh in horder])
    zc = 48 + (4 * g + 2 * pair) * 128
    m = {"qT": np.ascontiguousarray(q.transpose(2, 1, 0)),
         "kTs": np.ascontiguousarray(oa[:, 3072 + g * 128:3072 + (g + 1) * 128].T),
         "vs": np.ascontiguousarray(oa[:, 3584 + g * 128:3584 + (g + 1) * 128]),
         "kTw": np.ascontiguousarray(oa[:, 4096 + g * 128:4096 + (g + 1) * 128].T),
         "vw": np.ascontiguousarray(oa[:, 4608 + g * 128:4608 + (g + 1) * 128]),
         "kcT": np.ascontiguousarray(kcb[g].T), "vc": np.ascontiguousarray(vcb[g]),
         "glog": np.ascontiguousarray(ob[tok][:, gcols]),
         "z": np.ascontiguousarray(ob[tok][:, zc:zc + 256]),
         "bonus": np.ascontiguousarray(bonus.reshape(16, 4, 128, 128)[list(chunks)].reshape(-1, 128, 128))}
    m.update(cst)
    return m


def run_nsa_attn(oa, ob, kcb, vcb):
    cst, bonus = nsa_consts()
    chunks = list(range(16))
    nc = _get("nsa_attn", lambda: build_nsa_attn_prog(chunks))
    maps = [nsa_core_inputs(oa, ob, kcb, vcb, c // 2, c % 2, chunks, cst, bonus) for c in range(NCORES)]
    res = run_bass_kernel_spmd(nc, maps, core_ids=list(range(NCORES)))
    og = np.concatenate([r["og"] for r in res.results], axis=1)
    return np.ascontiguousarray(og.T)


def nsa_consts():
    kk = np.arange(128)[:, None]
    qq = np.arange(512)[None, :]
    winm = np.zeros((8, 128, 512), np.float32)
    for oi in range(8):
        kp = 128 * (oi - 4) + kk
        dd = qq - kp
        winm[oi] = np.where((dd >= 0) & (dd < 512), 0.0, -1.0)
    cmpm = np.zeros((5, 128, 512), np.float32)
    for dl in range(5):
        cmpm[dl] = np.where(qq >= 16 * kk + 31 - 512 * dl, 0.0, -1.0)
    jj = np.arange(128)[:, None]
    ex = np.zeros((128, 64, 128), np.float32)
    for kt in range(64):
        ex[:, kt, :] = np.where(jj == 2 * kt + np.arange(128)[None, :] // 64, BIG, 0.0)
    ii = np.arange(512)[:, None]
    j2 = np.arange(128)[None, :]
    agg = ((ii >= 4 * j2 - 1) & (ii <= 4 * j2 + 3) & (ii < 511)).astype(np.float32)
    t = np.arange(S)[:, None]
    cur = t // 64
    allowed = j2 * 64 <= t
    forced = (j2 == 0) | (j2 == cur) | (j2 == cur - 1)
    bonus = np.where(allowed, np.where(forced, 1.0e4, 0.0), NEG).astype(np.float32).reshape(64, 128, 128)
    ident = np.eye(128)
    return {"winm": _bf16(winm), "cmpm": _bf16(cmpm), "ex": _bf16(ex), "agg": _bf16(agg),
            "ident": _bf16(ident), "bigi": _bf16(ident * BIG)}, bonus


def build_nsa_attn_prog(chunks, debug=False):
    nc = bass.Bass("TRN2", target_bir_lowering=False)
    nchunks = len(chunks)
    NQ = nchunks * 512
    qT = nc.dram_tensor("qT", [128, 4, NQ], BF16, kind="ExternalInput").ap()
    kTs_d = nc.dram_tensor("kTs", [128, S], BF16, kind="ExternalInput").ap()
    kTw_d = nc.dram_tensor("kTw", [128, S], BF16, kind="ExternalInput").ap()
    vs_d = nc.dram_tensor("vs", [S, 128], BF16, kind="ExternalInput").ap()
    vw_d = nc.dram_tensor("vw", [S, 128], BF16, kind="ExternalInput").ap()
    kcT_d = nc.dram_tensor("kcT", [128, 512], F32, kind="ExternalInput").ap()
    vc_d = nc.dram_tensor("vc", [512, 128], F32, kind="ExternalInput").ap()
    glog_d = nc.dram_tensor("glog", [NQ, 12], F32, kind="ExternalInput").ap()
    z_d = nc.dram_tensor("z", [NQ, 256], F32, kind="ExternalInput").ap()
    bonus_d = nc.dram_tensor("bonus", [NQ // 128, 128, 128], F32, kind="ExternalInput").ap()
    winm_d = nc.dram_tensor("winm", [8, 128, 512], BF16, kind="ExternalInput").ap()
    cmpm_d = nc.dram_tensor("cmpm", [5, 128, 512], BF16, kind="ExternalInput").ap()
    ex_d = nc.dram_tensor("ex", [128, 64, 128], BF16, kind="ExternalInput").ap()
    agg_d = nc.dram_tensor("agg", [512, 128], BF16, kind="ExternalInput").ap()
    ident_d = nc.dram_tensor("ident", [128, 128], BF16, kind="ExternalInput").ap()
    bigi_d = nc.dram_tensor("bigi", [128, 128], BF16, kind="ExternalInput").ap()
    og = nc.dram_tensor("og", [NQ, 256], BF16, kind="ExternalOutput").ap()
    if debug:
        dbg_sel = nc.dram_tensor("dbg_sel", [128, 512], BF16, kind="ExternalOutput").ap()
        dbg_imp = nc.dram_tensor("dbg_imp", [128, 4, 128], F32, kind="ExternalOutput").ap()
    P = Prog(nc)
    kTs = P.sbuf("kTs_sb", [128, S], BF16)
    kTw = P.sbuf("kTw_sb", [128, S], BF16)
    vsa = P.sbuf("vsa", [128, 64, 129], BF16)
    vwa = P.sbuf("vwa", [128, 64, 129], BF16)
    kcTf = P.sbuf("kcTf", [128, 512], F32)
    kcT = P.sbuf("kcT_sb", [128, 512], BF16)
    vcf = P.sbuf("vcf", [128, 4, 128], F32)
    vca = P.sbuf("vca", [128, 4, 257], BF16)
    winm = P.sbuf("winm_sb", [128, 8, 512], BF16)
    cmpm = P.sbuf("cmpm_sb", [128, 5, 512], BF16)
    ex = P.sbuf("ex_sb", [128, 64, 128], BF16)
    id_sb = P.sbuf("id_sb", [128, 128], BF16)
    bi_sb = P.sbuf("bi_sb", [128, 128], BF16)
    qc = [P.sbuf(f"qc{i}", [128, 4, 512], BF16) for i in range(2)]
    gl = P.sbuf("gl", [128, 4, 12], F32)
    gs = P.sbuf("gs", [128, 4, 12], F32)
    zf = P.sbuf("zf", [128, 4, 256], F32)
    zs = P.sbuf("zs", [128, 4, 256], F32)
    bon = P.sbuf("bon", [128, 4, 128], F32)
    Ec = [P.sbuf(f"Ec{i}", [128, 512], BF16) for i in range(4)]
    Eb = [P.sbuf(f"Eb{i}", [128, 512], BF16) for i in range(3)]
    csb = [P.sbuf(f"csb{i}", [128, 257], F32) for i in range(2)]
    vsb = [P.sbuf(f"vsb{i}", [128, 2, 129], F32) for i in range(4)]
    rc = [P.sbuf(f"rc{i}", [128, 1], F32) for i in range(4)]
    rg = [P.sbuf(f"rg{i}", [128, 1], F32) for i in range(4)]
    acc = P.sbuf("acc", [128, 4, 256], F32)
    imp = P.sbuf("imp", [128, 4, 128], F32)
    impb = P.sbuf("impb", [128, 128], F32)
    wrk = P.sbuf("wrk", [128, 128], F32)
    m8a = P.sbuf("m8a", [128, 8], F32)
    m8b = P.sbuf("m8b", [128, 8], F32)
    self_ = P.sbuf("self", [128, 128], F32)
    selm = P.sbuf("selm", [128, 128], BF16)
    selT = P.sbuf("selT", [128, 512], BF16)
    ogb = [P.sbuf(f"ogb{i}", [128, 256], BF16) for i in range(2)]
    psS = [P.psum(f"psS{i}", [128, 512]) for i in range(2)]
    psC = P.psum("psC", [128, 257])
    psV = [[P.psum(f"psV{a}_{i}", [128, 2, 129]) for i in range(2)] for a in range(2)]
    psT = P.psum("psT", [128, 128], BF16)

    for q4 in range(4):
        cs = slice(q4 * 2048, (q4 + 1) * 2048)
        P.dma(kTs[:, cs], kTs_d[:, cs], writes=[f"kTs{q4}"])
        P.dma(kTw[:, cs], kTw_d[:, cs], writes=[f"kTw{q4}"])
    kTs_keys = [f"kTs{i}" for i in range(4)]
    kTw_keys = [f"kTw{i}" for i in range(4)]
    vs_v = vs_d.rearrange("(t p) d -> p t d", p=128)
    vw_v = vw_d.rearrange("(t p) d -> p t d", p=128)
    for q4 in range(4):
        P.dma(vsa[:, q4 * 16:(q4 + 1) * 16, 0:128], vs_v[:, q4 * 16:(q4 + 1) * 16, :], writes=[f"vsa_{q4}"])
        P.dma(vwa[:, q4 * 16:(q4 + 1) * 16, 0:128], vw_v[:, q4 * 16:(q4 + 1) * 16, :], writes=[f"vwa_{q4}"])
    P.op("pool", lambda e: e.memset(vsa[:, :, 128:129], 1.0), writes=["vsa1"])
    P.op("pool", lambda e: e.memset(vwa[:, :, 128:129], 1.0), writes=["vwa1"])
    P.dma(kcTf[:], kcT_d, writes=["kcTf"])
    P.op("dve", lambda e: e.tensor_copy(out=kcT[:], in_=kcTf[:]), reads=["kcTf"], writes=["kcT"])
    P.dma(vcf[:], vc_d.rearrange("(t p) d -> p t d", p=128), writes=["vcf"])
    P.op("dve", lambda e: e.tensor_copy(out=vca[:, :, 0:128], in_=vcf[:]), reads=["vcf"], writes=["vca0"])
    P.dma(vca[:, :, 128:256], agg_d.rearrange("(t p) j -> p t j", p=128), writes=["vca1"])
    P.op("pool", lambda e: e.memset(vca[:, :, 256:257], 1.0), writes=["vca2"])
    vca_keys = ["vca0", "vca1", "vca2"]
    P.dma(winm[:], winm_d.rearrange("m p n -> p m n"), writes=["winm"])
    P.dma(cmpm[:], cmpm_d.rearrange("m p n -> p m n"), writes=["cmpm"])
    P.dma(ex[:], ex_d, writes=["ex"])
    P.dma(id_sb[:], ident_d, writes=["id"])
    P.dma(bi_sb[:], bigi_d, writes=["bi"])

    def load_q(i):
        P.dma(qc[i % 2][:], qT[:, :, i * 512:(i + 1) * 512], writes=[f"qc{i % 2}"])

    sidx = [0]
    vidx = [0]

    def score_tile(masks, kT_ap, q_ap, kkeys, qkey):
        b = sidx[0] % 2
        sidx[0] += 1
        first = True
        for (ml, mr, mkeys) in masks:
            P.op("pe", lambda e, ml=ml, mr=mr, b=b, first=first: e.matmul(psS[b][:], lhsT=ml, rhs=mr, start=first, stop=False),
                 reads=mkeys, writes=[f"psS{b}"])
            first = False
        P.op("pe", lambda e, b=b, first=first, kT_ap=kT_ap, q_ap=q_ap: e.matmul(psS[b][:], lhsT=kT_ap, rhs=q_ap, start=first, stop=True),
             reads=kkeys + [qkey], writes=[f"psS{b}"])
        return b

    load_q(0)
    eidx = 0
    for i in range(nchunks):
        c = chunks[i]
        u = i % 2
        qk = f"qc{u}"
        if i + 1 < nchunks:
            load_q(i + 1)
        rows = slice(i * 512, (i + 1) * 512)
        P.dma(gl[:], glog_d[rows, :].rearrange("(t p) c -> p t c", p=128), writes=["gl"])
        P.op("act", lambda e: e.activation(out=gs[:], in_=gl[:], func=AF.Sigmoid), reads=["gl"], writes=["gs"])
        P.dma(zf[:], z_d[rows, :].rearrange("(t p) c -> p t c", p=128), writes=["zf"])
        P.op("act", lambda e: e.activation(out=zs[:], in_=zf[:], func=AF.Silu), reads=["zf"], writes=["zs"])
        P.dma(bon[:], bonus_d[4 * i:4 * i + 4].rearrange("t p j -> p t j"), writes=["bon"])
        ntn = min(3, c // 4) + 1
        for hh in range(4):
            for tn in range(ntn):
                dl = c - 4 * tn
                masks = [(bi_sb[:], cmpm[:, dl, :], ["bi", "cmpm"])] if dl <= 4 else []
                b = score_tile(masks, kcT[:, tn * 128:(tn + 1) * 128], qc[u][:, hh, :], ["kcT"], qk)
                P.op("act", lambda e, b=b, tn=tn: e.activation(out=Ec[tn][:], in_=psS[b][:], func=AF.Exp, scale=NSA_SCALE),
                     reads=[f"psS{b}"], writes=[f"Ec{tn}"])
            for qt in range(4):
                cb = (hh * 4 + qt) % 2
                r4 = qt
                for tn in range(ntn):
                    P.op("pe", lambda e, tn=tn, qt=qt, s0=(tn == 0), s1=(tn == ntn - 1): e.matmul(
                        psC[:], lhsT=Ec[tn][:, qt * 128:(qt + 1) * 128], rhs=vca[:, tn, :], start=s0, stop=s1),
                        reads=[f"Ec{tn}"] + vca_keys, writes=["psC"])
                P.op("act", lambda e, cb=cb: e.copy(out=csb[cb][:], in_=psC[:]), reads=["psC"], writes=[f"csb{cb}"])
                P.op("dve", lambda e, cb=cb, r4=r4: e.tensor_scalar(out=rc[r4][:], in0=csb[cb][:, 256:257], scalar1=1.0e-30,
                                                                    scalar2=None, op0=ALU.max),
                     reads=[f"csb{cb}"], writes=[f"rc{r4}"])
                P.op("dve", lambda e, r4=r4: e.reciprocal(out=rc[r4][:], in_=rc[r4][:]), reads=[f"rc{r4}"], writes=[f"rc{r4}"])
                if hh < 2:
                    P.op("dve", lambda e, r4=r4, qt=qt, hh=hh: e.tensor_tensor(out=rg[r4][:], in0=rc[r4][:], in1=gs[:, qt, hh:hh + 1],
                                                                               op=ALU.mult),
                         reads=[f"rc{r4}", "gs"], writes=[f"rg{r4}"])
                    P.op("dve", lambda e, cb=cb, r4=r4, qt=qt, hh=hh: e.tensor_scalar(
                        out=acc[:, qt, hh * 128:(hh + 1) * 128], in0=csb[cb][:, 0:128], scalar1=rg[r4][:, 0:1], scalar2=None,
                        op0=ALU.mult), reads=[f"csb{cb}", f"rg{r4}"], writes=[f"acc{qt}_{hh}"])
                if hh == 0:
                    P.op("dve", lambda e, cb=cb, r4=r4, qt=qt: e.tensor_scalar(
                        out=imp[:, qt, :], in0=csb[cb][:, 128:256], scalar1=rc[r4][:, 0:1], scalar2=None, op0=ALU.mult),
                        reads=[f"csb{cb}", f"rc{r4}"], writes=[f"imp{qt}"])
                else:
                    P.op("dve", lambda e, cb=cb, r4=r4, qt=qt: e.scalar_tensor_tensor(
                        out=imp[:, qt, :], in0=csb[cb][:, 128:256], scalar=rc[r4][:, 0:1], in1=imp[:, qt, :],
                        op0=ALU.mult, op1=ALU.add), reads=[f"csb{cb}", f"rc{r4}", f"imp{qt}"], writes=[f"imp{qt}"])
        for qt in range(4):
            P.op("dve", lambda e, qt=qt: e.tensor_tensor(out=impb[:], in0=imp[:, qt, :], in1=bon[:, qt, :], op=ALU.add),
                 reads=[f"imp{qt}", "bon"], writes=["impb"])
            P.op("dve", lambda e: e.max(out=m8a[:], in_=impb[:]), reads=["impb"], writes=["m8a"])
            P.op("dve", lambda e: e.match_replace(out=wrk[:], in_to_replace=m8a[:], in_values=impb[:], imm_value=NEG),
                 reads=["impb", "m8a"], writes=["wrk"])
            P.op("dve", lambda e: e.max(out=m8b[:], in_=wrk[:]), reads=["wrk"], writes=["m8b"])
            P.op("dve", lambda e: e.tensor_scalar(out=self_[:], in0=impb[:], scalar1=m8b[:, 7:8], scalar2=-1.0,
                                                  op0=ALU.is_ge, op1=ALU.add), reads=["impb", "m8b"], writes=["self"])
            P.op("dve", lambda e: e.tensor_copy(out=selm[:], in_=self_[:]), reads=["self"], writes=["selm"])
            P.op("pe", lambda e: e.transpose(out=psT[:], in_=selm[:], identity=id_sb[:]), reads=["selm", "id"], writes=["psT"])
            P.op("dve", lambda e, qt=qt: e.tensor_copy(out=selT[:, qt * 128:(qt + 1) * 128], in_=psT[:]),
                 reads=["psT"], writes=["selT"])
        if debug:
            P.dma(dbg_sel, selT[:], reads=["selT"])
            P.dma(dbg_imp, imp[:], reads=[f"imp{qt}" for qt in range(4)])
        for br in (1, 2):
            if br == 1:
                kts = list(range(4 * c + 4))
            else:
                kts = list(range(max(0, 4 * c - 4), 4 * c + 4))
            kT_sb, vaug = (kTs, vsa) if br == 1 else (kTw, vwa)
            kkeys = kTs_keys if br == 1 else kTw_keys
            vkeys = ([f"vsa_{i}" for i in range(4)] + ["vsa1"]) if br == 1 else ([f"vwa_{i}" for i in range(4)] + ["vwa1"])
            for hh in range(2):
                vset = vidx[0] % 2
                vidx[0] += 1
                pend = None
                for ki, kt in enumerate(kts):
                    masks = []
                    if br == 1:
                        masks.append((ex[:, kt, :], selT[:], ["ex", "selT"]))
                    if kt >= 4 * c - 4 and (br == 2 or kt >= 4 * c):
                        masks.append((bi_sb[:], winm[:, kt - 4 * c + 4, :], ["bi", "winm"]))
                    b = score_tile(masks, kT_sb[:, kt * 128:(kt + 1) * 128], qc[u][:, hh, :], kkeys, qk)
                    eb = eidx % 3
                    eidx += 1
                    P.op("act", lambda e, b=b, eb=eb: e.activation(out=Eb[eb][:], in_=psS[b][:], func=AF.Exp, scale=NSA_SCALE),
                         reads=[f"psS{b}"], writes=[f"Eb{eb}"])

                    def pv(eb=eb, kt=kt, ki=ki, vset=vset, vaug=vaug, nk=len(kts), vkeys=vkeys):
                        for qt in range(4):
                            bk, q2 = qt // 2, qt % 2
                            st_ = (ki == 0 and q2 == 0)
                            sp_ = (ki == nk - 1 and q2 == 1)
                            P.op("pe", lambda e, eb=eb, qt=qt, bk=bk, q2=q2, kt=kt, st_=st_, sp_=sp_, vaug=vaug, vset=vset: e.matmul(
                                psV[vset][bk][:, q2, :], lhsT=Eb[eb][:, qt * 128:(qt + 1) * 128], rhs=vaug[:, kt, :],
                                start=st_, stop=sp_),
                                reads=[f"Eb{eb}"] + vkeys, writes=[f"psV{vset}_{bk}"])
                    if pend is not None:
                        pend()
                    pend = pv
                pend()
                for bk in range(2):
                    sb_ = vset * 2 + bk
                    P.op("act", lambda e, bk=bk, vset=vset, sb_=sb_: e.copy(out=vsb[sb_][:].rearrange("p a d -> p (a d)"),
                                                                  in_=psV[vset][bk][:].rearrange("p a d -> p (a d)")),
                         reads=[f"psV{vset}_{bk}"], writes=[f"vsb{sb_}"])
                    for q2 in range(2):
                        qt = bk * 2 + q2
                        r4 = qt
                        P.op("dve", lambda e, sb_=sb_, q2=q2, r4=r4: e.reciprocal(out=rc[r4][:], in_=vsb[sb_][:, q2, 128:129]),
                             reads=[f"vsb{sb_}"], writes=[f"rc{r4}"])
                        P.op("dve", lambda e, r4=r4, qt=qt, hh=hh, br=br: e.tensor_tensor(
                            out=rg[r4][:], in0=rc[r4][:], in1=gs[:, qt, br * 4 + hh:br * 4 + hh + 1], op=ALU.mult),
                            reads=[f"rc{r4}", "gs"], writes=[f"rg{r4}"])
                        P.op("dve", lambda e, sb_=sb_, q2=q2, r4=r4, qt=qt, hh=hh: e.scalar_tensor_tensor(
                            out=acc[:, qt, hh * 128:(hh + 1) * 128], in0=vsb[sb_][:, q2, 0:128], scalar=rg[r4][:, 0:1],
                            in1=acc[:, qt, hh * 128:(hh + 1) * 128], op0=ALU.mult, op1=ALU.add),
                            reads=[f"vsb{sb_}", f"rg{r4}", f"acc{qt}_{hh}"], writes=[f"acc{qt}_{hh}"])
        for qt in range(4):
            ob_ = qt % 2
            P.op("pool", lambda e, qt=qt, ob_=ob_: e.tensor_tensor(out=ogb[ob_][:], in0=acc[:, qt, :], in1=zs[:, qt, :], op=ALU.mult),
                 reads=[f"acc{qt}_{hh}" for hh in range(2)] + ["zs"], writes=[f"ogb{ob_}"])
            P.dma(og[i * 512 + qt * 128:i * 512 + (qt + 1) * 128, :], ogb[ob_][:], reads=[f"ogb{ob_}"])
    P.finish()
    return nc


def kernel(x, positions, hgrn_lb_logits,
           l0_w_in, l0_g_norm, l0_w_out, l0_ln_g, l0_ln_b,
           l1_w_in, l1_cmp_pos_k, l1_cmp_w1_k, l1_cmp_w2_k,
           l1_cmp_pos_v, l1_cmp_w1_v, l1_cmp_w2_v, l1_w_out, l1_ln_g, l1_ln_b,
           l2_w_in, l2_sinks, l2_w_out, l2_ln_g, l2_ln_b,
           l3_w_in, l3_g_norm, l3_w_out, l3_ln_g, l3_ln_b):
    f = lambda a: np.ascontiguousarray(np.asarray(a, dtype=np.float32))
    positions = np.ascontiguousarray(np.asarray(positions, dtype=np.int32))
    lbl = f(hgrn_lb_logits)
    h = f(x)[0]
    ogT = run_hgrn(np.ascontiguousarray(h.T), f(l0_w_in), f(l0_g_norm), lbl, False)
    h = run_out(ogT, h, f(l0_w_out), f(l0_ln_g), f(l0_ln_b))
    oa, ob = run_nsa_proj(np.ascontiguousarray(h.T), f(l1_w_in), positions)
    kc = oa[:, 2048:2560].reshape(S, 4, 128)
    vc = oa[:, 2560:3072].reshape(S, 4, 128)
    kcb, vcb = run_nsa_cmp(kc, vc, f(l1_cmp_pos_k), f(l1_cmp_w1_k), f(l1_cmp_w2_k),
                           f(l1_cmp_pos_v), f(l1_cmp_w1_v), f(l1_cmp_w2_v))
    ogT = run_nsa_attn(oa, ob, kcb, vcb)
    h = run_out(ogT, h, f(l1_w_out), f(l1_ln_g), f(l1_ln_b))
    ogT = run_swa(np.ascontiguousarray(h.T), f(l2_w_in), f(l2_sinks), positions)
    h = run_out(ogT, h, f(l2_w_out), f(l2_ln_g), f(l2_ln_b))
    ogT = run_hgrn(np.ascontiguousarray(h.T), f(l3_w_in), f(l3_g_norm), lbl, True)
    h = run_out(ogT, h, f(l3_w_out), f(l3_ln_g), f(l3_ln_b))
    return h[None].astype(np.float32)
```
